# Optimizing a Trainium2 kernel written in Bass

```python
import jax, jax.numpy as jnp
from jax import lax
import numpy as np

D_MODEL = 2048
BATCH = 4
SEQ = 2048
DEPTH = 1
DEC_BATCH = 4
DEC_SEQ = 4096
PAST_LEN = 128

N_MEM = 256
HEAD_DIM = 128
DIL_PAIRS = ((128, 1), (512, 4), (2048, 16))
HEADS_PER_GROUP = 4
N_DIL_HEADS = HEADS_PER_GROUP * len(DIL_PAIRS)
D_ATTN = N_DIL_HEADS * HEAD_DIM
D_ATTN_OUT = HEADS_PER_GROUP * HEAD_DIM
POOL_WINDOWS = (2, 4, 8, 16)
POOL_GROUP = 256
D_POOL = POOL_GROUP * len(POOL_WINDOWS)
N_XHEADS = 4
XHEAD_DIM = 256
D_XATTN = N_XHEADS * XHEAD_DIM
D_IN = 3 * D_ATTN + D_POOL + D_XATTN
N_BRANCH = 3
D_FF = 5504
EPS = 1e-6
NEG = -1e30

kernel_name = "hybrid_dilated_pool_memory_encoder"


def rms_norm(x, gain):
    x32 = x.astype(jnp.float32)
    y = x32 * lax.rsqrt(jnp.mean(x32 * x32, axis=-1, keepdims=True) + EPS)
    return (y * gain.astype(jnp.float32)).astype(x.dtype)


def swiglu(h, w_up, w_down):
    g, u = jnp.split(h @ w_up, 2, axis=-1)
    return (jax.nn.silu(g) * u) @ w_down


def alibi_slopes():
    s = 2.0 ** (-8.0 * np.arange(1, N_DIL_HEADS + 1) / N_DIL_HEADS)
    return s.reshape(HEADS_PER_GROUP, len(DIL_PAIRS)).T.astype(np.float32)


def dilated_window_attention(q, k, v, slopes, window, dilation):
    B, T, H, hd = q.shape
    d = dilation
    r = window // (2 * d)
    L = T // d
    nb = -(-L // r)
    Lp = nb * r

    def to_sub(a):
        a = a.reshape(B, L, d, H, hd).transpose(0, 2, 3, 1, 4)
        return jnp.pad(a, ((0, 0), (0, 0), (0, 0), (0, Lp - L), (0, 0)))

    def neighbours(a):
        a = jnp.pad(a, ((0, 0), (0, 0), (0, 0), (r, r), (0, 0))).reshape(B, d, H, nb + 2, r, hd)
        return jnp.concatenate([a[:, :, :, :-2], a[:, :, :, 1:-1], a[:, :, :, 2:]], axis=4)

    qb = to_sub(q).reshape(B, d, H, nb, r, hd)
    kb = neighbours(to_sub(k))
    vb = neighbours(to_sub(v))
    s = jnp.einsum('bdhnqc,bdhnkc->bdhnqk', qb, kb).astype(jnp.float32) * (hd ** -0.5)
    a_idx = jnp.arange(r)[:, None]
    c_idx = jnp.arange(3 * r)[None, :]
    rel = c_idx - r - a_idx
    kj = (jnp.arange(nb)[:, None, None] - 1) * r + c_idx[None]
    valid = (jnp.abs(rel) <= r)[None] & (kj >= 0) & (kj < L)
    dist = (d * jnp.abs(rel)).astype(jnp.float32)
    bias = -slopes[:, None, None] * dist[None]
    s = s + bias[None, None, :, None]
    s = jnp.where(valid[None, None, None], s, NEG)
    lse = jax.nn.logsumexp(s, axis=-1)
    p = jnp.exp(s - lse[..., None])
    o = jnp.einsum('bdhnqk,bdhnkc->bdhnqc', p.astype(v.dtype), vb)
    o = o.reshape(B, d, H, Lp, hd)[:, :, :, :L].transpose(0, 3, 1, 2, 4).reshape(B, T, H, hd)
    lse = lse.reshape(B, d, H, Lp)[..., :L].transpose(0, 3, 1, 2).reshape(B, T, H)
    return o, lse


def multiscale_pool(u, w_pool, pool_scale):
    B, T, _ = u.shape
    G = len(POOL_WINDOWS)
    ug = u.reshape(B, T, G, POOL_GROUP).astype(jnp.float32)
    cs = jnp.concatenate([jnp.zeros((B, 1, G, POOL_GROUP), jnp.float32), jnp.cumsum(ug, axis=1)], axis=1)
    t = jnp.arange(T)
    outs = []
    for g, w in enumerate(POOL_WINDOWS):
        lo = jnp.clip(t - w // 2, 0, T)
        hi = jnp.clip(t - w // 2 + w, 0, T)
        cs_g = cs[:, :, g]
        mean = (cs_g[:, hi] - cs_g[:, lo]) / (hi - lo).astype(jnp.float32)[None, :, None]
        outs.append(mean - ug[:, :, g])
    pooled = jnp.stack(outs, axis=2).astype(u.dtype)
    mixed = jnp.einsum('btgc,gce->btge', pooled, w_pool).reshape(B, T, D_POOL)
    return mixed * pool_scale


def memory_attention(q, mem_h, w_mem_kv):
    B, T, _ = q.shape
    M = mem_h.shape[1]
    k, v = jnp.split(mem_h @ w_mem_kv, 2, axis=-1)
    k = k.reshape(B, M, N_XHEADS, XHEAD_DIM)
    v = v.reshape(B, M, N_XHEADS, XHEAD_DIM)
    qh = q.reshape(B, T, N_XHEADS, XHEAD_DIM)
    s = jnp.einsum('bthc,bmhc->bhtm', qh, k).astype(jnp.float32) * (XHEAD_DIM ** -0.5)
    p = jax.nn.softmax(s, axis=-1).astype(v.dtype)
    return jnp.einsum('bhtm,bmhc->bthc', p, v).reshape(B, T, D_XATTN)


def encoder_layer(x, mem, ffn1_norm_pre, ffn1_w_up, ffn1_w_down, ffn1_norm_post,
                  mix_norm_pre, mem_norm, w_in, w_mem_kv, w_pool, pool_scale,
                  w_br_attn, w_br_pool, w_br_mem, w_gate, b_gate, w_out, mix_norm_post,
                  ffn2_norm_pre, ffn2_w_up, ffn2_w_down, ffn2_norm_post, final_norm):
    B, T, _ = x.shape
    x = x + 0.5 * rms_norm(swiglu(rms_norm(x, ffn1_norm_pre), ffn1_w_up, ffn1_w_down), ffn1_norm_post)
    h = rms_norm(x, mix_norm_pre)
    proj = h @ w_in
    qkv = proj[..., :3 * D_ATTN].reshape(B, T, 3, len(DIL_PAIRS), HEADS_PER_GROUP, HEAD_DIM)
    u_pool = proj[..., 3 * D_ATTN:3 * D_ATTN + D_POOL]
    q_mem = proj[..., 3 * D_ATTN + D_POOL:]
    slopes = alibi_slopes()
    outs, lses = [], []
    for g, (window, dilation) in enumerate(DIL_PAIRS):
        o, l = dilated_window_attention(qkv[:, :, 0, g], qkv[:, :, 1, g], qkv[:, :, 2, g],
                                        jnp.asarray(slopes[g]), window, dilation)
        outs.append(o)
        lses.append(l)
    alpha = jax.nn.softmax(jnp.stack(lses, axis=0), axis=0)
    y_attn = jnp.sum(alpha[..., None] * jnp.stack(outs, axis=0).astype(jnp.float32), axis=0)
    y_attn = y_attn.astype(x.dtype).reshape(B, T, D_ATTN_OUT)
    y_pool = multiscale_pool(u_pool, w_pool, pool_scale)
    y_mem = memory_attention(q_mem, rms_norm(mem, mem_norm), w_mem_kv)
    gates = jax.nn.sigmoid((h @ w_gate + b_gate).astype(jnp.float32)).astype(x.dtype)
    gates = gates.reshape(B, T, N_BRANCH, D_MODEL)
    merged = (gates[:, :, 0] * (y_attn @ w_br_attn)
              + gates[:, :, 1] * (y_pool @ w_br_pool)
              + gates[:, :, 2] * (y_mem @ w_br_mem))
    x = x + rms_norm(merged @ w_out, mix_norm_post)
    x = x + 0.5 * rms_norm(swiglu(rms_norm(x, ffn2_norm_pre), ffn2_w_up, ffn2_w_down), ffn2_norm_post)
    return rms_norm(x, final_norm)


def run_trunk(x, mem, params):
    for l in range(DEPTH):
        layer_params = [p[l] for p in params]
        x = encoder_layer(x, mem, *layer_params)
    return x


def setup_inputs(seed: int = 0) -> dict:
    key = jax.random.key(seed)
    ks = jax.random.split(key, 32)

    def w(k, shape, fan_in):
        return jax.random.normal(k, shape, jnp.float32) * fan_in ** -0.5

    def gain(k, shape):
        return 1.0 + 0.02 * jax.random.normal(k, shape, jnp.float32)

    L = DEPTH
    return {
        "x_prompt": jax.random.normal(ks[0], (BATCH, SEQ, D_MODEL), jnp.float32),
        "x_sample": jax.random.normal(ks[1], (DEC_BATCH, DEC_SEQ, D_MODEL), jnp.float32),
        "mem_prompt": jax.random.normal(ks[2], (BATCH, N_MEM, D_MODEL), jnp.float32),
        "mem_sample": jax.random.normal(ks[3], (DEC_BATCH, N_MEM, D_MODEL), jnp.float32),
        "ffn1_norm_pre": gain(ks[4], (L, D_MODEL)),
        "ffn1_w_up": w(ks[5], (L, D_MODEL, 2 * D_FF), D_MODEL),
        "ffn1_w_down": w(ks[6], (L, D_FF, D_MODEL), D_FF),
        "ffn1_norm_post": gain(ks[7], (L, D_MODEL)),
        "mix_norm_pre": gain(ks[8], (L, D_MODEL)),
        "mem_norm": gain(ks[9], (L, D_MODEL)),
        "w_in": w(ks[10], (L, D_MODEL, D_IN), D_MODEL),
        "w_mem_kv": w(ks[11], (L, D_MODEL, 2 * D_XATTN), D_MODEL),
        "w_pool": w(ks[12], (L, len(POOL_WINDOWS), POOL_GROUP, POOL_GROUP), POOL_GROUP),
        "pool_scale": gain(ks[13], (L, D_POOL)),
        "w_br_attn": w(ks[14], (L, D_ATTN_OUT, D_MODEL), D_ATTN_OUT),
        "w_br_pool": w(ks[15], (L, D_POOL, D_MODEL), D_POOL),
        "w_br_mem": w(ks[16], (L, D_XATTN, D_MODEL), D_XATTN),
        "w_gate": w(ks[17], (L, D_MODEL, N_BRANCH * D_MODEL), D_MODEL),
        "b_gate": 0.02 * jax.random.normal(ks[18], (L, N_BRANCH * D_MODEL), jnp.float32),
        "w_out": w(ks[19], (L, D_MODEL, D_MODEL), D_MODEL),
        "mix_norm_post": gain(ks[20], (L, D_MODEL)),
        "ffn2_norm_pre": gain(ks[21], (L, D_MODEL)),
        "ffn2_w_up": w(ks[22], (L, D_MODEL, 2 * D_FF), D_MODEL),
        "ffn2_w_down": w(ks[23], (L, D_FF, D_MODEL), D_FF),
        "ffn2_norm_post": gain(ks[24], (L, D_MODEL)),
        "final_norm": gain(ks[25], (L, D_MODEL)),
    }


def reference(x_prompt, x_sample, mem_prompt, mem_sample,
              ffn1_norm_pre, ffn1_w_up, ffn1_w_down, ffn1_norm_post,
              mix_norm_pre, mem_norm, w_in, w_mem_kv, w_pool, pool_scale,
              w_br_attn, w_br_pool, w_br_mem, w_gate, b_gate, w_out, mix_norm_post,
              ffn2_norm_pre, ffn2_w_up, ffn2_w_down, ffn2_norm_post, final_norm):
    params = [ffn1_norm_pre, ffn1_w_up, ffn1_w_down, ffn1_norm_post,
              mix_norm_pre, mem_norm, w_in, w_mem_kv, w_pool, pool_scale,
              w_br_attn, w_br_pool, w_br_mem, w_gate, b_gate, w_out, mix_norm_post,
              ffn2_norm_pre, ffn2_w_up, ffn2_w_down, ffn2_norm_post, final_norm]
    y_prompt = run_trunk(x_prompt, mem_prompt, params)
    y_sample = run_trunk(x_sample, mem_sample, params)
    return (y_prompt, y_sample)
```

```python
import contextlib
import numpy as np
import concourse.bass as bass
import concourse.mybir as mybir
from concourse.bass_utils import run_bass_kernel_spmd

F32 = mybir.dt.float32
BF16 = mybir.dt.bfloat16
AF = mybir.ActivationFunctionType
ALU = mybir.AluOpType

D = 2048
DC = 16
T = 4096
NT = 512
NTILES = T // NT
DFF = 5504
FC = DFF // 128
NMEM = 256
EPS = 1e-6
NB = 4
SLABW = 256
DILS = (1, 4, 16)
POOLW = (2, 4, 8, 16)

V_F1PRE, V_F1POST, V_MIXPRE, V_MEMN, V_PSCALE, V_BGATE, V_MIXPOST, V_F2PRE, V_F2POST, V_FINAL = (
    0, 16, 32, 48, 64, 72, 120, 136, 152, 168)
NV = 184

C_Q, C_K, C_V, C_U, C_QM = 0, 1536, 3072, 4608, 5632


class Prog:
    ENG = ("pe", "act", "dve", "pool", "sp")

    def __init__(self, nc, stack):
        self.nc = nc
        self.stack = stack
        self.lists = {e: [] for e in self.ENG}
        self.cnt = {e: 0 for e in self.ENG}
        self.csem = {e: stack.enter_context(nc.semaphore("c_" + e)) for e in ("pe", "act", "dve", "pool")}
        self.waited = {e: {} for e in self.ENG}
        self.dsems = {}
        self.last = {e: None for e in self.ENG}

    def wait(self, eng, tok):
        if tok is None:
            return
        if isinstance(tok, list):
            for t in tok:
                self.wait(eng, t)
            return
        sem, val, key = tok
        if self.waited[eng].get(key, 0) >= val:
            return
        self.waited[eng][key] = val
        self.lists[eng].append(("w", sem, val))

    def op(self, eng, fn, waits=(), sig=True):
        for t in waits:
            self.wait(eng, t)
        tok = None
        if sig:
            self.cnt[eng] += 1
            tok = (self.csem[eng], self.cnt[eng], "c_" + eng)
            self.last[eng] = tok
        self.lists[eng].append(("o", fn, sig))
        return tok

    def dma(self, q, out, in_, semname, waits=()):
        for t in waits:
            self.wait(q, t)
        if semname not in self.dsems:
            self.dsems[semname] = [self.stack.enter_context(self.nc.semaphore("d_" + semname)), 0]
        s = self.dsems[semname]
        s[1] += 16
        self.lists[q].append(("d", out, in_, s[0]))
        return (s[0], s[1], "d_" + semname)

    def bar(self):
        return [self.last[e] for e in ("pe", "act", "dve", "pool") if self.last[e] is not None]

    def emit(self):
        nc = self.nc
        with nc.Block() as block:
            def replay(eng_obj, name):
                for it in self.lists[name]:
                    if it[0] == "w":
                        eng_obj.wait_ge(it[1], it[2])
                    elif it[0] == "o":
                        ins = it[1](eng_obj)
                        if it[2]:
                            ins.then_inc(self.csem[name], 1)
                    else:
                        eng_obj.dma_start(out=it[1], in_=it[2]).then_inc(it[3], 16)

            @block.tensor
            def _(e):
                replay(e, "pe")

            @block.scalar
            def _(e):
                replay(e, "act")

            @block.vector
            def _(e):
                replay(e, "dve")

            @block.gpsimd
            def _(e):
                replay(e, "pool")

            @block.sync
            def _(e):
                replay(e, "sp")


class K:
    pass


def build_program(stop=None, dbg=False):
    nc = bass.Bass("TRN2", target_bir_lowering=False)
    stack = contextlib.ExitStack()
    k = K()
    k.nc = nc
    p = Prog(nc, stack)
    k.p = p

    def din(name, shape, dt=F32):
        return nc.dram_tensor(name, shape, dt, kind="ExternalInput").ap()

    x_d = din("x", [T, D])
    mem_d = din("mem", [NMEM, D])
    vecs_d = din("vecs", [128, NV])
    abias_d = din("abias", [128, 12 * 256])
    flag_d = din("flag", [128, 1])
    pedge_d = din("pedge", [128, 4 * 3 * 8])
    W = {}
    for name, shp in (("ffn1_w_up", [D, 2 * DFF]), ("ffn1_w_down", [DFF, D]), ("w_in", [D, 6656]),
                      ("w_mem_kv", [D, 2048]), ("w_pool", [1024, 256]), ("w_br_attn", [512, D]),
                      ("w_br_pool", [1024, D]), ("w_br_mem", [1024, D]), ("w_gate", [D, 3 * D]),
                      ("w_out", [D, D]), ("ffn2_w_up", [D, 2 * DFF]), ("ffn2_w_down", [DFF, D])):
        W[name] = din(name, shp)
    y_d = nc.dram_tensor("y", [T, D], F32, kind="ExternalOutput").ap()

    def dscr(name, shape, dt):
        if dbg:
            return nc.dram_tensor(name, shape, dt, kind="ExternalOutput").ap()
        return nc.dram_tensor(name, shape, dt).ap()

    qkT = dscr("s_qkT", [3072, T], BF16)
    vtok = dscr("s_vtok", [T, 1536], BF16)
    upT = dscr("s_upT", [1024, T], F32)
    qmT = dscr("s_qmT", [1024, T], BF16)
    x1T = dscr("s_x1T", [D, T], F32)
    yaT = dscr("s_yaT", [512, T], BF16)
    if dbg:
        d_kmem = dscr("d_kmem", [128, 8 * NMEM], BF16)
        d_vmem = dscr("d_vmem", [128, 2 * 1024], BF16)

    NJ_PA, NJ_PB = 94, 124
    wcache = nc.dram_tensor("s_wcache", [NJ_PA + NJ_PB, 128, 16 * SLABW], BF16).ap()

    def finalize():
        for name, s_ in p.dsems.items():
            p.wait("sp", (s_[0], s_[1], "d_" + name))
        p.wait("sp", p.bar())
        p.emit()
        return nc, stack

    def sb(name, shape, dt):
        return stack.enter_context(nc.sbuf_tensor("sb_" + name, shape, dt))

    slabs = [sb(f"slab{i}", [128, 16, SLABW], BF16) for i in range(NB)]
    tmpg = [sb(f"tmpg{i}", [128, 2, NT], F32) for i in range(2)]
    sq = [sb(f"sq{i}", [128, NT], BF16) for i in range(4)]
    rstd = [sb(f"rstd{i}", [128, NT], F32) for i in range(2)]
    stage16 = [sb(f"st16_{i}", [128, 2, NT], BF16) for i in range(2)]
    ident = sb("ident", [128, 128], F32)
    ones32 = sb("ones32", [128, 128], F32)
    ones16 = sb("ones16", [128, 128], BF16)
    vecs = sb("vecs", [128, NV], F32)
    wtab = sb("wtab", [128, 12, 256], F32)
    flag = sb("flag", [128, 1], F32)
    pedge = sb("pedge", [128, 4, 3, 8], F32)
    kmemT = sb("kmemT", [128, 8, NMEM], BF16)
    vmem = sb("vmem", [128, 2, 1024], BF16)
    wpool = sb("wpool", [128, 8, 256], BF16)
    ARENA = 125952
    arena = sb("arena", [128, ARENA // 4], F32)
    ps = stack.enter_context(nc.psum_tensor("ps", [128, 8, 512], F32))

    def carve(off, nbytes, dt, pattern=None, **kw):
        a = arena[:, off // 4:(off + nbytes) // 4]
        if dt != F32:
            a = a.bitcast(dt)
        if pattern:
            a = a.rearrange(pattern, **kw)
        return a

    O_XRES, O_HBF, O_HID, O_YBUF = 0, 32768, 49152, 93184
    xres = carve(O_XRES, 32768, F32, "p (c n) -> p c n", n=NT)
    hbf = carve(O_HBF, 16384, BF16, "p (c n) -> p c n", n=NT)
    hid = carve(O_HID, 44032, BF16, "p (c n) -> p c n", n=NT)
    ybuf = carve(O_YBUF, 32768, F32, "p (c n) -> p c n", n=NT)
    xin = carve(O_YBUF, 32768, F32, "p (b f) -> p b f", f=D)
    fin = carve(O_HID, 32768, F32, "p (c n) -> p c n", n=NT)

    k.slab_i = 0
    k.cache_mode = None
    k.cache_next = 0
    k.wb_q = []
    k.slab_wb = [None] * NB
    k.wb_tok = {}
    k.slabfree = [None] * NB
    k.grp_i = 0
    k.cur_grp = 0
    k.deferred = []
    k.brd = {b: [] for b in range(8)}
    k.sq_i = 0
    k.sq_free = [None] * 4
    k.rstd_i = 0
    k.tmpg_i = 0
    k.tmpg_free = [[], []]
    k.st16_i = 0
    k.st16_free = [None, None]

    def vcol(base, c):
        return vecs[:, base + c:base + c + 1]

    def flush_deferred():
        d = k.deferred
        k.deferred = []
        for fn in d:
            fn()

    def writeback_pending(keep=0):
        while len(k.wb_q) > keep:
            b, ld, cid, nk, ncols = k.wb_q.pop(0)
            t = p.dma("pool", wcache[cid, :, 0:nk * ncols].rearrange("p (kc n) -> p kc n", n=ncols),
                      slabs[b][:, 0:nk, 0:ncols], f"wb{b}", waits=[ld])
            k.slab_wb[b] = t
            k.wb_tok[cid] = t

    def issue_slab(j):
        b = k.slab_i % NB
        k.slab_i += 1
        nk, ncols, k0, c0 = j["nk"], j["ncols"], j["k0"], j["c0"]
        cid = None
        if k.cache_mode is not None:
            cid = k.cache_next
            k.cache_next += 1
        if k.cache_mode == "use":
            src = wcache[cid, :, 0:nk * ncols].rearrange("p (kc n) -> p kc n", n=ncols)
            ld = p.dma("pool", slabs[b][:, 0:nk, 0:ncols], src, f"slab{b}",
                       waits=[k.slabfree[b], k.slab_wb[b], k.wb_tok[cid]])
            return b, ld
        src = j["w"][k0 * 128:(k0 + nk) * 128, c0:c0 + ncols].rearrange("(kc p) n -> p kc n", p=128)
        ld = p.dma("pool", slabs[b][:, 0:nk, 0:ncols], src, f"slab{b}", waits=[k.slabfree[b], k.slab_wb[b]])
        if k.cache_mode == "fill":
            writeback_pending(keep=1)
            k.wb_q.append((b, ld, cid, nk, ncols))
        return b, ld

    def prefetch(jobs):
        for j in jobs[:NB]:
            j["pf"] = issue_slab(j)

    def do_job(j):
        b, ld = j["pf"] if "pf" in j else issue_slab(j)
        slab = slabs[b]
        nk, ncols, k0, c0 = j["nk"], j["ncols"], j["k0"], j["c0"]
        if j["start"]:
            g = k.grp_i % 3
            k.grp_i += 1
            k.cur_grp = g
            w_pe = [ld] + list(k.brd[2 * g]) + list(k.brd[2 * g + 1]) + list(j.get("waits", []))
        else:
            g = k.cur_grp
            w_pe = [ld] + list(j.get("waits", []))
        banks = (2 * g, 2 * g + 1)
        N = j["n"]
        act = j["act"]
        tok = None
        ops = []
        if j["mode"] == "A":
            nch = ncols // 128
            for i in range(nch):
                for kc in range(nk):
                    ops.append((ps[:, banks[i], 0:N], slab[:, kc, i * 128:(i + 1) * 128], act(k0 + kc),
                                j["start"] and kc == 0, j["stop"] and kc == nk - 1))
        else:
            ntb = N // 128
            for tb in range(ntb):
                for kc in range(nk):
                    ops.append((ps[:, banks[tb], 0:ncols] if ntb == 2 else ps[:, banks[tb // 2], (tb % 2) * 256:(tb % 2) * 256 + ncols],
                                act(k0 + kc)[:, tb * 128:(tb + 1) * 128], slab[:, kc, 0:ncols],
                                j["start"] and kc == 0, j["stop"] and kc == nk - 1))
        for idx, (o, l, r, st, sp_) in enumerate(ops):
            tok = p.op("pe", lambda e, o=o, l=l, r=r, st=st, sp_=sp_: e.matmul(o, l, r, start=st, stop=sp_),
                       waits=w_pe if idx == 0 else (), sig=(idx == len(ops) - 1))
        k.slabfree[b] = tok
        flush_deferred()
        if j["stop"]:
            readers = j["evac"](banks, tok)
            k.brd[banks[0]] = list(readers)
            k.brd[banks[1]] = list(readers)

    def mk_jobs(w, K_chunks, col_blocks, act, n, mode, evac_for, waits=()):
        jobs = []
        kgs = [(s, min(16, K_chunks - s)) for s in range(0, K_chunks, 16)]
        first = True
        for bi, (c0, ncols) in enumerate(col_blocks):
            for gi, (k0, nk) in enumerate(kgs):
                jobs.append(dict(w=w, k0=k0, nk=nk, c0=c0, ncols=ncols, act=act, n=n, mode=mode,
                                 start=(gi == 0), stop=(gi == len(kgs) - 1), evac=evac_for(bi),
                                 waits=list(waits) if first else []))
                first = False
        return jobs

    def stats_add(src_ap, N, first, last, src_waits, on="act"):
        i = k.sq_i % 4
        k.sq_i += 1
        sqt = sq[i]
        if on == "act":
            t_sq = p.op("act", lambda e: e.activation(out=sqt[:, 0:N], in_=src_ap, func=AF.Square),
                        waits=list(src_waits) + [k.sq_free[i]])
        else:
            t_sq = p.op("dve", lambda e: e.tensor_tensor(out=sqt[:, 0:N], in0=src_ap, in1=src_ap, op=ALU.mult),
                        waits=list(src_waits) + [k.sq_free[i]])

        def pe_part():
            w = [t_sq] + (list(k.brd[6]) if first else [])
            t = p.op("pe", lambda e: e.matmul(ps[:, 6, 0:N], ones16[:, :], sqt[:, 0:N], start=first, stop=last),
                     waits=w)
            k.sq_free[i] = t
            k.stats_tok = t
        return t_sq, pe_part

    def compute_rstd(N):
        i = k.rstd_i % 2
        k.rstd_i += 1
        r = rstd[i]
        t1 = p.op("act", lambda e: e.activation(out=r[:, 0:N], in_=ps[:, 6, 0:N], func=AF.Sqrt, bias=eps_col[:, 0:1],
                                                scale=1.0 / D),
                  waits=[k.stats_tok] + p.bar())
        k.brd[6] = [t1]
        t2 = p.op("dve", lambda e: e.reciprocal(out=r[:, 0:N], in_=r[:, 0:N]), waits=[t1])
        return r, t2

    def prenorm(src, gbase, dst, N, src_waits, nchunks=DC, cw=None):
        for c in range(nchunks):
            _, pe_part = stats_add(src[:, c, 0:N], N, c == 0, c == nchunks - 1,
                                   [cw[c]] if cw is not None else src_waits, on="act")
            pe_part()
        r, t_r = compute_rstd(N)
        toks = []
        for c in range(nchunks):
            toks.append(p.op("dve", lambda e, c=c: e.scalar_tensor_tensor(
                out=dst[:, c, 0:N], in0=src[:, c, 0:N], scalar=vcol(gbase, c), in1=r[:, 0:N],
                op0=ALU.mult, op1=ALU.mult), waits=[t_r] + list(src_waits)))
        return toks[-1]

    def out_evac_for(nchunk_total):
        def evac_for(bi):
            def evac(banks, pe_tok):
                readers = []
                defs = []
                for i in range(2):
                    c = bi * 2 + i
                    t_cp = p.op("dve", lambda e, c=c, i=i: e.tensor_copy(ybuf[:, c, :], ps[:, banks[i], :]),
                                waits=[pe_tok])
                    t_sq, pe_part = stats_add(ybuf[:, c, :], NT, c == 0, c == nchunk_total - 1, [t_cp], on="act")
                    readers += [t_cp]
                    defs.append(pe_part)
                k.deferred += defs
                return readers
            return evac
        return evac_for

    def post_residual(gbase):
        flush_deferred()
        r, t_r = compute_rstd(NT)
        tl = None
        k.res_cw = []
        for c in range(DC):
            t1 = p.op("pool" if c % 2 == 0 else "dve",
                      lambda e, c=c: e.tensor_tensor(out=ybuf[:, c, :], in0=ybuf[:, c, :], in1=r[:, :], op=ALU.mult),
                      waits=[t_r])
            tl = p.op("dve", lambda e, c=c: e.scalar_tensor_tensor(
                out=xres[:, c, :], in0=ybuf[:, c, :], scalar=vcol(gbase, c), in1=xres[:, c, :],
                op0=ALU.mult, op1=ALU.add), waits=[t1])
            k.res_cw.append(tl)
        return tl

    def ffn(w_up, w_down, g_pre, g_post, x_waits, cw=None):
        t_h = prenorm(xres, g_pre, hbf, NT, x_waits, cw=cw)
        pairs = [(pi * 256, min(256, DFF - pi * 256)) for pi in range((DFF + 255) // 256)]

        def evac_g_for(bi):
            def evac(banks, pe_tok):
                i = k.tmpg_i % 2
                k.tmpg_i += 1
                k.cur_tmpg = i
                nch = pairs[bi][1] // 128
                toks = []
                for ch in range(nch):
                    toks.append(p.op("act", lambda e, ch=ch, i=i: e.activation(out=tmpg[i][:, ch, :], in_=ps[:, banks[ch], :],
                                                                               func=AF.Silu),
                                     waits=[pe_tok] + list(k.tmpg_free[i])))
                k.tmpg_tok = toks[-1]
                return toks
            return evac

        def evac_u_for(bi):
            def evac(banks, pe_tok):
                i = k.cur_tmpg
                nch = pairs[bi][1] // 128
                toks = []
                for ch in range(nch):
                    c = bi * 2 + ch
                    toks.append(p.op("dve", lambda e, ch=ch, i=i, c=c: e.tensor_tensor(
                        out=hid[:, c, :], in0=tmpg[i][:, ch, :], in1=ps[:, banks[ch], :], op=ALU.mult),
                        waits=[pe_tok, k.tmpg_tok]))
                k.tmpg_free[i] = [toks[-1]]
                k.hid_tok = toks[-1]
                return toks
            return evac

        first = True
        for bi, (c0, ncols) in enumerate(pairs):
            for which in (0, 1):
                do_job(dict(w=w_up, k0=0, nk=16, c0=c0 + which * DFF, ncols=ncols, act=lambda kc: hbf[:, kc, :], n=NT,
                            mode="A", start=True, stop=True, evac=(evac_g_for if which == 0 else evac_u_for)(bi),
                            waits=[t_h] if first else []))
                first = False
        jobs = mk_jobs(w_down, FC, [(cb * 256, 256) for cb in range(8)], lambda kc: hid[:, kc, :], NT, "A",
                       out_evac_for(DC), waits=[k.hid_tok])
        for j in jobs:
            do_job(j)
        return post_residual(g_post)

    eps_col = sb("eps_col", [128, 1], F32)
    t_c = []
    t_c.append(p.dma("sp", vecs[:, :], vecs_d, "c0"))
    t_c.append(p.dma("sp", wtab[:, :, :], abias_d.rearrange("p (g n) -> p g n", n=256), "c0"))
    t_c.append(p.dma("sp", flag[:, :], flag_d, "c0"))
    t_c.append(p.dma("sp", pedge[:, :, :, :], pedge_d.rearrange("p (g r n) -> p g r n", g=4, r=3), "c0"))
    t_wp = p.dma("pool", wpool[:, :, :], W["w_pool"].rearrange("(gi p) n -> p gi n", p=128), "c1")
    p.op("pool", lambda e: e.memset(ident[:, :], 0.0))
    p.op("pool", lambda e: e.affine_select(out=ident[:, :], in_=ident[:, :], compare_op=ALU.not_equal, fill=1.0,
                                           base=0, pattern=[[-1, 128]], channel_multiplier=1), waits=[p.last["pool"]])
    p.op("pool", lambda e: e.memset(ones32[:, :], 1.0))
    p.op("pool", lambda e: e.memset(ones16[:, :], 1.0))
    p.op("pool", lambda e: e.memset(eps_col[:, :], EPS))
    t_const = p.last["pool"]
    t_w = p.op("dve", lambda e: e.tensor_scalar(out=wtab[:, :, :], in0=wtab[:, :, :], scalar1=float(128 ** 0.5),
                                                scalar2=None, op0=ALU.mult), waits=[t_c[-1]])
    t_v = p.op("dve", lambda e: e.tensor_scalar(out=vecs[:, V_F1POST:V_F1POST + 16], in0=vecs[:, V_F1POST:V_F1POST + 16],
                                                scalar1=0.5, scalar2=None, op0=ALU.mult), waits=[t_c[-1]])
    t_v = p.op("dve", lambda e: e.tensor_scalar(out=vecs[:, V_F2POST:V_F2POST + 16], in0=vecs[:, V_F2POST:V_F2POST + 16],
                                                scalar1=0.5, scalar2=None, op0=ALU.mult), waits=[t_v])
    setup_toks = [t_const, t_w, t_v, t_wp]
    k.stats_tok = None
    if stop == "P0":
        return finalize()

    memin = carve(O_YBUF, 16384, F32, "p (b f) -> p b f", f=D)
    memT = carve(O_XRES, 16384, F32, "p (c n) -> p c n", n=NMEM)
    memh = carve(O_HBF, 8192, BF16, "p (c n) -> p c n", n=NMEM)
    t_ld = p.dma("sp", memin[:, :, :], mem_d.rearrange("(b p) f -> p b f", p=128), "xin")
    tr_toks = []
    for c in range(DC):
        bank = 6 + (c % 2)
        tp = None
        for mb in range(2):
            tp = p.op("pe", lambda e, c=c, mb=mb, bank=bank: e.transpose(ps[:, bank, mb * 128:(mb + 1) * 128],
                                                                         memin[:, mb, c * 128:(c + 1) * 128], ident[:, :]),
                      waits=[t_ld] + setup_toks + list(k.brd[bank]), sig=(mb == 1))
        eng = "act" if c % 2 else "dve"
        if eng == "act":
            tcp = p.op("act", lambda e, c=c, bank=bank: e.copy(out=memT[:, c, :], in_=ps[:, bank, 0:NMEM]), waits=[tp])
        else:
            tcp = p.op("dve", lambda e, c=c, bank=bank: e.tensor_copy(memT[:, c, :], ps[:, bank, 0:NMEM]), waits=[tp])
        k.brd[bank] = [tcp]
        tr_toks.append(tcp)
    if stop == "PMT":
        return finalize()
    t_mh = prenorm(memT, V_MEMN, memh, NMEM, tr_toks[-2:])
    if stop == "PMN":
        return finalize()

    def evac_kmem_for(bi):
        def evac(banks, pe_tok):
            toks = []
            for i in range(2):
                toks.append(p.op("act" if i else "dve",
                                 (lambda e, i=i: e.copy(out=kmemT[:, bi * 2 + i, :], in_=ps[:, banks[i], 0:NMEM])) if i else
                                 (lambda e, i=i: e.tensor_copy(kmemT[:, bi * 2 + i, :], ps[:, banks[i], 0:NMEM])),
                                 waits=[pe_tok]))
            return toks
        return evac

    def evac_vmem_for(bi):
        def evac(banks, pe_tok):
            toks = []
            for mb in range(2):
                src = ps[:, banks[mb], 0:256]
                dst = vmem[:, mb, bi * 256:(bi + 1) * 256]
                toks.append(p.op("act" if mb else "dve",
                                 (lambda e, s=src, d_=dst: e.copy(out=d_, in_=s)) if mb else
                                 (lambda e, s=src, d_=dst: e.tensor_copy(d_, s)), waits=[pe_tok]))
            return toks
        return evac

    for j in mk_jobs(W["w_mem_kv"], 16, [(cb * 256, 256) for cb in range(4)], lambda kc: memh[:, kc, :], NMEM, "A",
                     evac_kmem_for, waits=[t_mh]):
        do_job(j)
    if stop == "PMK":
        flush_deferred()
        return finalize()
    if stop == "PMKK":
        for j in mk_jobs(W["w_mem_kv"], 16, [(cb * 256, 256) for cb in range(4)], lambda kc: memh[:, kc, :], NMEM, "A",
                         evac_kmem_for):
            do_job(j)
        flush_deferred()
        return finalize()
    for j in mk_jobs(W["w_mem_kv"], 16, [(1024 + cb * 256, 256) for cb in range(4)], lambda kc: memh[:, kc, :], NMEM, "B",
                     evac_vmem_for):
        do_job(j)
    flush_deferred()

    if dbg:
        p.dma("sp", d_kmem, kmemT[:, :, :].rearrange("p a b -> p (a b)"), "dbg", waits=p.bar())
        p.dma("sp", d_vmem, vmem[:, :, :].rearrange("p a b -> p (a b)"), "dbg", waits=p.bar())
    if stop == "PM":
        return finalize()
    scr_w = {"qkT": [], "vtok": [], "upT": [], "qmT": [], "x1T": [], "yaT": []}
    xin_free = p.bar()
    xres_free = p.bar()
    def pa_load(ti_, free_w):
        t0_ = ti_ * NT
        return p.dma("sp", xin[:, :, :], x_d[t0_:t0_ + NT, :].rearrange("(b p) f -> p b f", p=128), "xin", waits=free_w)

    def pa_transposes(t_ld, xres_free):
        tr = []
        for c in range(DC):
            bank = 6 + (c % 2)
            tp = None
            for tb in range(4):
                tp = p.op("pe", lambda e, c=c, tb=tb, bank=bank: e.transpose(ps[:, bank, tb * 128:(tb + 1) * 128],
                                                                             xin[:, tb, c * 128:(c + 1) * 128], ident[:, :]),
                          waits=[t_ld] + list(k.brd[bank]), sig=(tb == 3))
            if c % 2:
                tcp = p.op("act", lambda e, c=c, bank=bank: e.copy(out=xres[:, c, :], in_=ps[:, bank, :]),
                           waits=[tp] + xres_free)
            else:
                tcp = p.op("dve", lambda e, c=c, bank=bank: e.tensor_copy(xres[:, c, :], ps[:, bank, :]),
                           waits=[tp] + xres_free)
            k.brd[bank] = [tcp]
            tr.append(tcp)
        return tr

    tr_toks = pa_transposes(pa_load(0, xin_free), xres_free)
    for ti in range(NTILES):
        t0 = ti * NT
        writeback_pending()
        k.cache_mode = "fill" if ti == 0 else "use"
        k.cache_next = 0
        t_x1 = ffn(W["ffn1_w_up"], W["ffn1_w_down"], V_F1PRE, V_F1POST, tr_toks[-2:])
        t_sp = p.dma("sp", x1T.rearrange("(c p) t -> p c t", p=128)[:, :, t0:t0 + NT], xres[:, :, :], "x1T", waits=[t_x1])
        scr_w["x1T"] = [t_sp]
        t_h = prenorm(xres, V_MIXPRE, hbf, NT, [t_x1], cw=list(k.res_cw))
        xin_free = p.bar()
        xres_free = [t_sp] + p.bar()
        t_ld_next = pa_load(ti + 1, xin_free) if ti + 1 < NTILES else None

        def evac_fm16_for(dst_scr, row0, key):
            def evac_for(bi):
                def evac(banks, pe_tok):
                    i = k.st16_i % 2
                    k.st16_i += 1
                    st = stage16[i]
                    t_a = p.op("act", lambda e: e.copy(out=st[:, 0, :], in_=ps[:, banks[0], :]),
                               waits=[pe_tok, k.st16_free[i]])
                    t_b = p.op("dve", lambda e: e.tensor_copy(st[:, 1, :], ps[:, banks[1], :]),
                               waits=[pe_tok, k.st16_free[i]])
                    r0 = row0 + bi * 256
                    t_d = p.dma("sp", dst_scr[r0:r0 + 256, t0:t0 + NT].rearrange("(c p) t -> p c t", p=128), st[:, :, :],
                                f"st16_{i}", waits=[t_a, t_b])
                    k.st16_free[i] = t_d
                    scr_w[key] = [kk for kk in scr_w[key] if kk[2] != t_d[2]] + [t_d]
                    return [t_a, t_b]
                return evac
            return evac_for

        def evac_v_for(bi):
            def evac(banks, pe_tok):
                i = k.st16_i % 2
                k.st16_i += 1
                st = stage16[i].rearrange("p a (b c) -> p (a b) c", c=256)
                toks = []
                for tb in range(4):
                    src = ps[:, banks[tb // 2], (tb % 2) * 256:(tb % 2) * 256 + 256]
                    if tb // 2:
                        toks.append(p.op("act", lambda e, s=src, tb=tb: e.copy(out=st[:, tb, :], in_=s),
                                         waits=[pe_tok, k.st16_free[i]]))
                    else:
                        toks.append(p.op("dve", lambda e, s=src, tb=tb: e.tensor_copy(st[:, tb, :], s),
                                         waits=[pe_tok, k.st16_free[i]]))
                t_d = p.dma("sp", vtok[t0:t0 + NT, bi * 256:(bi + 1) * 256].rearrange("(b p) c -> p b c", p=128), st,
                            f"st16_{i}", waits=toks)
                k.st16_free[i] = t_d
                scr_w["vtok"] = [kk for kk in scr_w["vtok"] if kk[2] != t_d[2]] + [t_d]
                return toks
            return evac

        def evac_u_for(bi):
            def evac(banks, pe_tok):
                i = k.tmpg_i % 2
                k.tmpg_i += 1
                st = tmpg[i]
                t_a = p.op("act", lambda e: e.copy(out=st[:, 0, :], in_=ps[:, banks[0], :]),
                           waits=[pe_tok] + list(k.tmpg_free[i]))
                t_b = p.op("dve", lambda e: e.tensor_copy(st[:, 1, :], ps[:, banks[1], :]),
                           waits=[pe_tok] + list(k.tmpg_free[i]))
                r0 = bi * 256
                t_d = p.dma("sp", upT[r0:r0 + 256, t0:t0 + NT].rearrange("(c p) t -> p c t", p=128), st[:, :, :],
                            f"tmpg_{i}", waits=[t_a, t_b])
                k.tmpg_free[i] = [t_d]
                scr_w["upT"] = [kk for kk in scr_w["upT"] if kk[2] != t_d[2]] + [t_d]
                return [t_a, t_b]
            return evac

        hact = lambda kc: hbf[:, kc, :]
        for j in mk_jobs(W["w_in"], 16, [(C_Q + cb * 256, 256) for cb in range(12)], hact, NT, "A",
                         evac_fm16_for(qkT, 0, "qkT"), waits=[t_h]):
            do_job(j)
        if t_ld_next is not None:
            tr_toks = pa_transposes(t_ld_next, xres_free)
        for j in mk_jobs(W["w_in"], 16, [(C_V + cb * 256, 256) for cb in range(6)], hact, NT, "B", evac_v_for):
            do_job(j)
        for j in mk_jobs(W["w_in"], 16, [(C_U + cb * 256, 256) for cb in range(4)], hact, NT, "A", evac_u_for):
            do_job(j)
        for j in mk_jobs(W["w_in"], 16, [(C_QM + cb * 256, 256) for cb in range(4)], hact, NT, "A",
                         evac_fm16_for(qmT, 0, "qmT")):
            do_job(j)
        flush_deferred()
        if stop == "PA1":
            writeback_pending()
            return finalize()

    writeback_pending()
    assert k.cache_next == NJ_PA, k.cache_next
    k.cache_mode = None
    if stop == "PA":
        return finalize()
    flush_deferred()
    QT = [carve(0 + i * 8192, 8192, BF16) for i in range(2)]
    KT = [carve(16384 + i * 8192, 8192, BF16) for i in range(2)]
    VT = [carve(32768 + i * 8192, 8192, BF16, "p (c n) -> p c n", n=128) for i in range(2)]
    num = carve(49152, 16384, F32)
    den = carve(65536, 16384, F32)
    yab = carve(81920, 8192, BF16)
    Eb = [carve(90112 + i * 1024, 1024, F32) for i in range(4)]
    Pb = [carve(94208 + i * 512, 512, BF16) for i in range(4)]
    att_start = p.bar() + [scr_w["x1T"][0]]
    qkv_free = [att_start, att_start]
    blk_i = 0
    s_rd = [[], [], [], []]
    o_rd = [[], [], [], []]
    e_free = [None] * 4
    p_free = [None] * 4
    gh_i = 0
    scale = 128 ** -0.5
    t_yab = None
    for hh in range(4):
        t_z1 = p.op("pool", lambda e: e.memset(num[:, :], 0.0), waits=att_start + ([t_yab] if t_yab else []))
        t_z2 = p.op("pool", lambda e: e.memset(den[:, :], 0.0), waits=att_start + ([t_yab] if t_yab else []))
        acc_tok = [t_z1, t_z2]
        for g in range(3):
            d = DILS[g]
            L = T // d
            nch = L // 128
            gi = g * 4 + hh
            bsel = gh_i % 2
            gh_i += 1
            row = g * 512 + hh * 128
            wq = list(qkv_free[bsel]) + scr_w["qkT"] + scr_w["vtok"]
            lq = p.dma("sp", QT[bsel][:, :], qkT[row:row + 128, :], f"qkv{bsel}", waits=wq)
            lk = p.dma("sp", KT[bsel][:, :], qkT[1536 + row:1536 + row + 128, :], f"qkv{bsel}")
            lv = None
            vsrc = vtok[:, row:row + 128].rearrange("(jk p r) c -> r p jk c", p=128, r=d)
            for r in range(d):
                for j0 in range(0, nch, 8):
                    j1 = min(nch, j0 + 8)
                    lv = p.dma("sp", VT[bsel][:, r * nch + j0:r * nch + j1, :], vsrc[r][:, j0:j1, :], f"qkv{bsel}")
            ld_tok = lv
            jb = (2048 // d) // 128
            pend = None
            blocks = [(r, j) for r in range(d) for j in range(nch + 1)]

            def emit_s(r, j, slot):
                qa = 64 if j == 0 else 0
                qb = 64 if j == nch else 128
                qs = r + d * (128 * j - 64 + qa)
                nq = qb - qa
                qap = QT[bsel][:, qs:qs + d * (nq - 1) + 1:d]
                hv = [(half, jk) for half, jk in ((0, j - 1), (1, j)) if 0 <= jk < nch]
                c0_ = hv[0][0] * 128 + qa
                c1_ = hv[-1][0] * 128 + qb
                p.op("pe", lambda e, c0_=c0_, c1_=c1_, gi=gi: e.matmul(ps[:, 2 * slot, c0_:c1_], ident[:, :], wtab[:, gi, c0_:c1_],
                                                                        start=True, stop=False),
                     waits=[ld_tok] + list(s_rd[slot]) + setup_toks, sig=False)
                tk = None
                for n_, (half, jk) in enumerate(hv):
                    ks = r + d * 128 * jk
                    kap = KT[bsel][:, ks:ks + d * 127 + 1:d]
                    o = ps[:, 2 * slot, half * 128 + qa:half * 128 + qb]
                    last = (n_ == len(hv) - 1)
                    tk = p.op("pe", lambda e, o=o, kap=kap, qap=qap, last=last: e.matmul(o, kap, qap, start=False, stop=last),
                              sig=last)
                return tk

            def emit_rest(r, j, slot, t_s):
                qa = 64 if j == 0 else 0
                qb = 64 if j == nch else 128
                E, P_ = Eb[slot], Pb[slot]
                halves = [h for h in (0, 1) if 0 <= j - 1 + h < nch]
                c0_ = halves[0] * 128 + qa
                c1_ = halves[-1] * 128 + qb
                flagged = (j == jb and 1 in halves)
                if not flagged:
                    t_p = p.op("act", lambda e: e.activation(out=P_[:, c0_:c1_], in_=ps[:, 2 * slot, c0_:c1_], func=AF.Exp,
                                                             scale=scale), waits=[t_s, p_free[slot]])
                    s_rd[slot] = [t_p]
                else:
                    t_e = p.op("act", lambda e: e.activation(out=E[:, c0_:c1_], in_=ps[:, 2 * slot, c0_:c1_], func=AF.Exp,
                                                             scale=scale), waits=[t_s, e_free[slot]])
                    s_rd[slot] = [t_e]
                    t_p = None
                    for h in halves:
                        cs = slice(h * 128 + qa, h * 128 + qb)
                        if h == 1:
                            t_p = p.op("dve", lambda e, cs=cs: e.tensor_scalar(
                                out=P_[:, cs], in0=E[:, cs], scalar1=flag[:, 0:1], scalar2=None, op0=ALU.mult),
                                waits=[t_e, p_free[slot]])
                        else:
                            t_p = p.op("dve", lambda e, cs=cs: e.tensor_copy(P_[:, cs], E[:, cs]), waits=[t_e, p_free[slot]])
                    e_free[slot] = t_p
                nq = qb - qa
                t_pv = None
                for which in (0, 1):
                    for n_, h in enumerate(halves):
                        jk = j - 1 + h
                        cs = slice(h * 128 + qa, h * 128 + qb)
                        lhs = VT[bsel][:, r * nch + jk, :] if which == 0 else ones16[:, :]
                        o = ps[:, 2 * slot + 1, which * 128 + qa:which * 128 + qb]
                        t_pv = p.op("pe", lambda e, o=o, lhs=lhs, cs=cs, n_=n_: e.matmul(
                            o, lhs, P_[:, cs], start=(n_ == 0), stop=(n_ == len(halves) - 1)),
                            waits=[t_p] + list(o_rd[slot]), sig=True)
                p_free[slot] = t_pv
                qs = r + d * (128 * j - 64 + qa)
                sl = slice(qs, qs + d * (nq - 1) + 1, d)
                t_n = p.op("dve", lambda e: e.tensor_tensor(out=num[:, sl], in0=num[:, sl],
                                                            in1=ps[:, 2 * slot + 1, qa:qb], op=ALU.add),
                           waits=[t_pv] + acc_tok)
                t_d = p.op("dve", lambda e: e.tensor_tensor(out=den[:, sl], in0=den[:, sl],
                                                            in1=ps[:, 2 * slot + 1, 128 + qa:128 + qb], op=ALU.add),
                           waits=[t_pv] + acc_tok)
                o_rd[slot] = [t_n, t_d]
                return t_d

            last_tok = None
            pendq = []
            for (r, j) in blocks:
                slot = blk_i % 4
                blk_i += 1
                t_s = emit_s(r, j, slot)
                pendq.append((r, j, slot, t_s))
                if len(pendq) > 3:
                    last_tok = emit_rest(*pendq.pop(0))
            while pendq:
                last_tok = emit_rest(*pendq.pop(0))
            acc_tok = [last_tok]
            qkv_free[bsel] = [p.last["pe"], last_tok]
        t_r = p.op("dve", lambda e: e.reciprocal(out=den[:, :], in_=den[:, :]), waits=acc_tok)
        t_y = p.op("dve", lambda e: e.tensor_tensor(out=yab[:, :], in0=num[:, :], in1=den[:, :], op=ALU.mult),
                   waits=[t_r, t_yab])
        t_yab = p.dma("sp", yaT[hh * 128:(hh + 1) * 128, :], yab[:, :], "yab", waits=[t_y])
        att_start = [t_y]
    scr_w["yaT"] = [t_yab]

    if stop == "PATT":
        return finalize()
    H0 = O_HID
    Y0 = O_YBUF
    merged = carve(H0, 32768, F32, "p (c n) -> p c n", n=NT)
    utg = [[carve(H0 + i * 12672 + a * 4224, 4224, F32, "p (c n) -> p c n", n=528) for a in range(3)] for i in range(2)]
    Pm = [carve(H0 + 25344 + i * 2048, 2048, BF16, "p (c n) -> p c n", n=NT) for i in range(2)]
    rden = carve(H0 + 25344 + 4096, 2048, F32)
    ya = carve(H0 + 32768, 4096, BF16, "p (c n) -> p c n", n=NT)
    pooled = [carve(H0 + 36864 + i * 2048, 2048, BF16, "p (c n) -> p c n", n=NT) for i in range(2)]
    ypool = carve(Y0, 8192, BF16, "p (c n) -> p c n", n=NT)
    ymem = carve(Y0 + 8192, 8192, BF16, "p (c n) -> p c n", n=NT)
    qm = carve(Y0 + 16384, 8192, BF16, "p (c n) -> p c n", n=NT)
    xout = xin
    tile_free = p.bar() + [t_yab]
    xres_free2 = list(tile_free)
    fin_free = list(tile_free)
    utg_free = [[], []]
    pooled_free = [[], []]
    pm_free = [[], []]
    out_toks = []
    for ti in range(NTILES):
        t0 = ti * NT
        writeback_pending()
        k.cache_mode = "fill" if ti == 0 else "use"
        k.cache_next = NJ_PA
        wr = scr_w["x1T"] + scr_w["upT"] + scr_w["qmT"] + scr_w["yaT"]
        if ti == 0:
            t_x = p.dma("sp", xres[:, :, :], x1T.rearrange("(c p) t -> p c t", p=128)[:, :, t0:t0 + NT], "ldx",
                        waits=xres_free2 + wr)
        else:
            t_x = t_x_next
        t_ya = p.dma("sp", ya[:, :, :], yaT.rearrange("(c p) t -> p c t", p=128)[:, :, t0:t0 + NT], "ldya", waits=fin_free)
        t_h = prenorm(xres, V_MIXPRE, hbf, NT, [t_x])
        brjobs = []
        for br_, (wbr_, kcb_, ybr_) in enumerate(((W["w_br_attn"], 4, ya), (W["w_br_pool"], 8, ypool), (W["w_br_mem"], 8, ymem))):
            for cb_ in range(8):
                brjobs.append(dict(w=W["w_gate"], k0=0, nk=16, c0=br_ * D + cb_ * 256, ncols=256,
                                   act=lambda kc: hbf[:, kc, :], n=NT, mode="A", start=True, stop=True))
                brjobs.append(dict(w=wbr_, k0=0, nk=kcb_, c0=cb_ * 256, ncols=256,
                                   act=lambda kc, ybr_=ybr_: ybr_[:, kc, :], n=NT, mode="A", start=True, stop=True))
        prefetch(brjobs)
        lo = max(0, t0 - 8)
        hi_ = min(T, t0 + NT + 8)
        for g in range(4):
            w_ = POOLW[g]
            hw = w_ // 2
            bi = g % 2
            U, SA, SB = utg[bi]
            free_w = fin_free + utg_free[bi]
            t_u = p.dma("sp", U[:, :, lo - (t0 - 8):hi_ - (t0 - 8)],
                        upT[g * 256:(g + 1) * 256, lo:hi_].rearrange("(c p) t -> p c t", p=128), f"ldu{bi}", waits=free_w)
            tcur = [t_u]
            if ti == 0:
                tcur.append(p.op("pool", lambda e, U=U: e.memset(U[:, :, 0:8], 0.0), waits=free_w))
            if ti == NTILES - 1:
                tcur.append(p.op("pool", lambda e, U=U: e.memset(U[:, :, 520:528], 0.0), waits=free_w))
            src = U
            width = 528
            bufs = [SA, SB]
            for s_ in range(g + 1):
                sh = 1 << s_
                width -= sh
                dst = bufs[s_ % 2]
                tn = p.op("pool", lambda e, dst=dst, src=src, sh=sh, width=width: e.tensor_tensor(
                    out=dst[:, :, 0:width], in0=src[:, :, 0:width], in1=src[:, :, sh:sh + width], op=ALU.add),
                    waits=tcur + free_w)
                tcur = [tn]
                src = dst
            S = src
            base = 8 - hw
            edges = []
            if ti == 0:
                edges.append((0, 0))
            if ti == 3:
                edges.append((1, NT - 8))
            if ti == NTILES - 1:
                edges.append((2, NT - 8))
            for (reg, col) in edges:
                for ch in range(2):
                    tn = p.op("pool", lambda e, S=S, ch=ch, reg=reg, col=col, g=g, base=base: e.tensor_tensor(
                        out=S[:, ch, base + col:base + col + 8], in0=S[:, ch, base + col:base + col + 8],
                        in1=pedge[:, g, reg, :], op=ALU.mult), waits=tcur)
                    tcur = [tn]
            pb_ = pooled[bi]
            t_pl = p.op("dve", lambda e, S=S, U=U, pb_=pb_, base=base, w_=w_: e.scalar_tensor_tensor(
                out=pb_[:, :, :], in0=S[:, :, base:base + NT], scalar=1.0 / w_, in1=U[:, :, 8:8 + NT],
                op0=ALU.mult, op1=ALU.subtract), waits=tcur + pooled_free[bi] + fin_free)
            utg_free[bi] = [t_pl]
            tm = None
            for oc in range(2):
                bank = 7
                for ic in range(2):
                    tm = p.op("pe", lambda e, oc=oc, ic=ic, g=g, bank=bank, pb_=pb_: e.matmul(
                        ps[:, bank, :], wpool[:, g * 2 + ic, oc * 128:(oc + 1) * 128], pb_[:, ic, :],
                        start=(ic == 0), stop=(ic == 1)),
                        waits=[t_pl] + setup_toks + list(k.brd[bank]), sig=(ic == 1))
                te = p.op("act", lambda e, oc=oc, g=g, bank=bank: e.activation(
                    out=ypool[:, g * 2 + oc, :], in_=ps[:, bank, :], func=AF.Copy,
                    scale=vcol(V_PSCALE, g * 2 + oc)), waits=[tm] + tile_free)
                k.brd[bank] = [te]
            pooled_free[bi] = [tm]
        t_qm = p.dma("sp", qm[:, :, :], qmT.rearrange("(c p) t -> p c t", p=128)[:, :, t0:t0 + NT], "ldq", waits=tile_free)
        for hd in range(4):
            pm = Pm[hd % 2]
            tE = []
            for mc in range(2):
                tS = None
                sbk = (0 if hd % 2 == 0 else 4) + mc
                for cc in range(2):
                    tS = p.op("pe", lambda e, mc=mc, cc=cc, hd=hd, sbk=sbk: e.matmul(
                        ps[:, sbk, :], kmemT[:, hd * 2 + cc, mc * 128:(mc + 1) * 128], qm[:, hd * 2 + cc, :],
                        start=(cc == 0), stop=(cc == 1)), waits=[t_qm] + list(k.brd[sbk]), sig=(cc == 1))
                te = p.op("act", lambda e, mc=mc, pm=pm, sbk=sbk: e.activation(out=pm[:, mc, :], in_=ps[:, sbk, :], func=AF.Exp,
                                                                               scale=1.0 / 16.0),
                          waits=[tS] + pm_free[hd % 2] + tile_free)
                k.brd[sbk] = [te]
                tE.append(te)
            tO = []
            for cc in range(2):
                t_ = None
                for mc in range(2):
                    t_ = p.op("pe", lambda e, mc=mc, cc=cc, hd=hd, pm=pm: e.matmul(
                        ps[:, 2 + cc, :], vmem[:, mc, hd * 256 + cc * 128:hd * 256 + (cc + 1) * 128], pm[:, mc, :],
                        start=(mc == 0), stop=(mc == 1)), waits=tE + list(k.brd[2 + cc]), sig=(mc == 1))
                tO.append(t_)
            tD = None
            for mc in range(2):
                tD = p.op("pe", lambda e, mc=mc, pm=pm: e.matmul(ps[:, 6, :], ones16[:, :], pm[:, mc, :],
                                                               start=(mc == 0), stop=(mc == 1)),
                          waits=tE + list(k.brd[6]), sig=(mc == 1))
            pm_free[hd % 2] = [tD]
            t_r = p.op("dve", lambda e: e.reciprocal(out=rden[:, :], in_=ps[:, 6, :]), waits=[tD] + tile_free)
            k.brd[6] = [t_r]
            for cc in range(2):
                t_o = p.op("dve", lambda e, cc=cc, hd=hd: e.tensor_tensor(
                    out=ymem[:, hd * 2 + cc, :], in0=ps[:, 2 + cc, :], in1=rden[:, :], op=ALU.mult),
                    waits=[tO[cc], t_r])
                k.brd[2 + cc] = [t_o]
        mix_ready = p.bar()
        brs = [(W["w_br_attn"], 4, ya), (W["w_br_pool"], 8, ypool), (W["w_br_mem"], 8, ymem)]
        first = True
        for br, (wbr, kcb, ybr) in enumerate(brs):
            for cb in range(8):
                def evac_gate(banks, pe_tok, br=br, cb=cb):
                    i = k.tmpg_i % 2
                    k.tmpg_i += 1
                    k.cur_tmpg = i
                    toks = []
                    for ch in range(2):
                        col = br * 16 + cb * 2 + ch
                        toks.append(p.op("act", lambda e, ch=ch, i=i, col=col: e.activation(
                            out=tmpg[i][:, ch, :], in_=ps[:, banks[ch], :], func=AF.Sigmoid, bias=vcol(V_BGATE, col)),
                            waits=[pe_tok] + list(k.tmpg_free[i])))
                    k.tmpg_tok = toks[-1]
                    return toks

                def evac_br(banks, pe_tok, br=br, cb=cb):
                    i = k.cur_tmpg
                    toks = []
                    for ch in range(2):
                        c = cb * 2 + ch
                        if br == 0:
                            t_ = p.op("dve", lambda e, ch=ch, i=i, c=c: e.tensor_tensor(
                                out=merged[:, c, :], in0=tmpg[i][:, ch, :], in1=ps[:, banks[ch], :], op=ALU.mult),
                                waits=[pe_tok, k.tmpg_tok] + mix_ready)
                            toks.append(t_)
                        else:
                            t1 = p.op("dve", lambda e, ch=ch, i=i: e.tensor_tensor(
                                out=tmpg[i][:, ch, :], in0=tmpg[i][:, ch, :], in1=ps[:, banks[ch], :], op=ALU.mult),
                                waits=[pe_tok, k.tmpg_tok])
                            t2 = p.op("dve", lambda e, ch=ch, i=i, c=c: e.tensor_tensor(
                                out=merged[:, c, :], in0=merged[:, c, :], in1=tmpg[i][:, ch, :], op=ALU.add),
                                waits=[t1, k.merged_tok[c]])
                            toks.append(t1)
                            t_ = t2
                        k.merged_tok[c] = t_
                        k.tmpg_free[i] = [t_]
                    return toks

                if not hasattr(k, "merged_tok"):
                    k.merged_tok = [None] * DC
                jg = brjobs[(br * 8 + cb) * 2]
                jg.update(evac=evac_gate, waits=([t_h] + mix_ready) if first else [])
                do_job(jg)
                first = False
                jb_ = brjobs[(br * 8 + cb) * 2 + 1]
                jb_.update(evac=evac_br, waits=[t_ya] + mix_ready)
                do_job(jb_)
        flush_deferred()
        pe_done = p.last["pe"]
        t_mb = None
        for c in range(DC):
            if c % 2:
                t_mb = p.op("act", lambda e, c=c: e.copy(out=hbf[:, c, :], in_=merged[:, c, :]),
                            waits=[pe_done, k.merged_tok[c]])
            else:
                t_mb = p.op("dve", lambda e, c=c: e.tensor_copy(hbf[:, c, :], merged[:, c, :]),
                            waits=[pe_done, k.merged_tok[c]])
        mb_toks = [p.last["act"], p.last["dve"]]
        for j in mk_jobs(W["w_out"], 16, [(cb * 256, 256) for cb in range(8)], lambda kc: hbf[:, kc, :], NT, "A",
                         out_evac_for(DC), waits=mb_toks):
            do_job(j)
        t_x2 = post_residual(V_MIXPOST)
        t_x3 = ffn(W["ffn2_w_up"], W["ffn2_w_down"], V_F2PRE, V_F2POST, [t_x2], cw=list(k.res_cw))
        fcw = list(k.res_cw)
        for c in range(DC):
            _, pe_part = stats_add(xres[:, c, :], NT, c == 0, c == DC - 1, [fcw[c]], on="act")
            pe_part()
        r, t_r = compute_rstd(NT)
        t_f = None
        for c in range(DC):
            t_f = p.op("dve", lambda e, c=c, r=r: e.scalar_tensor_tensor(
                out=fin[:, c, :], in0=xres[:, c, :], scalar=vcol(V_FINAL, c), in1=r[:, :],
                op0=ALU.mult, op1=ALU.mult), waits=[t_r, t_x3])
        xres_free2 = [t_f, p.last["pe"], p.last["act"]]
        if ti + 1 < NTILES:
            t_x_next = p.dma("sp", xres[:, :, :], x1T.rearrange("(c p) t -> p c t", p=128)[:, :, t0 + NT:t0 + 2 * NT], "ldx",
                             waits=xres_free2)
        cp_toks = []
        bi_ = 0
        for tb in range(4):
            for c4 in range(4):
                bank = bi_ % 8
                bi_ += 1
                tp = None
                for cc in range(4):
                    c = c4 * 4 + cc
                    tp = p.op("pe", lambda e, c=c, tb=tb, cc=cc, bank=bank: e.transpose(
                        ps[:, bank, cc * 128:(cc + 1) * 128], fin[:, c, tb * 128:(tb + 1) * 128], ident[:, :]),
                        waits=[t_f] + list(k.brd[bank]), sig=(cc == 3))
                if bi_ % 2:
                    tcp = p.op("act", lambda e, tb=tb, c4=c4, bank=bank: e.copy(
                        out=xout[:, tb, c4 * 512:(c4 + 1) * 512], in_=ps[:, bank, :]), waits=[tp, t_f])
                else:
                    tcp = p.op("dve", lambda e, tb=tb, c4=c4, bank=bank: e.tensor_copy(
                        xout[:, tb, c4 * 512:(c4 + 1) * 512], ps[:, bank, :]), waits=[tp, t_f])
                k.brd[bank] = [tcp]
                cp_toks.append(tcp)
        fin_free = [p.last["pe"]]
        t_out = p.dma("sp", y_d[t0:t0 + NT, :].rearrange("(b p) f -> p b f", p=128), xout[:, :, :], "yout",
                      waits=cp_toks[-2:])
        out_toks = [t_out]
        tile_free = p.bar() + [t_out]
        if stop == "PB1":
            writeback_pending()
            return finalize()

    writeback_pending()
    assert k.cache_next == NJ_PA + NJ_PB, k.cache_next
    p.wait("sp", out_toks)
    p.wait("sp", p.bar())
    p.emit()
    return nc, stack


_CACHE = {}


def _consts():
    s = 2.0 ** (-8.0 * np.arange(1, 13) / 12.0)
    slopes = s.reshape(4, 3).T.astype(np.float32)
    kk = np.arange(128)[:, None].astype(np.float64)
    qq = np.arange(128)[None, :].astype(np.float64)
    ab = np.full((128, 12, 256), -30000.0, np.float32)
    for g in range(3):
        for hh in range(4):
            sl = float(slopes[g, hh]) * DILS[g]
            lo = -sl * np.abs(kk - qq - 64)
            hi = -sl * np.abs(kk - qq + 64)
            ab[:, g * 4 + hh, 0:128] = np.where(kk >= qq, lo, -30000.0)
            ab[:, g * 4 + hh, 128:256] = np.where(kk <= qq, hi, -30000.0)
    return ab.reshape(128, 12 * 256)


def _pedge(tend_mid):
    pe = np.ones((4, 3, 8), np.float32)
    for g, w in enumerate(POOLW):
        hw = w // 2
        for i in range(8):
            t = i
            cnt = min(T, t - hw + w) - max(0, t - hw)
            pe[g, 0, i] = w / cnt
            t = T - 8 + i
            cnt = min(T, t - hw + w) - max(0, t - hw)
            pe[g, 2, i] = w / cnt
            if tend_mid is not None:
                t = tend_mid - 8 + i
                cnt = min(tend_mid, t - hw + w) - max(0, t - hw)
                pe[g, 1, i] = w / cnt
    return np.ascontiguousarray(np.broadcast_to(pe.reshape(1, 96), (128, 96)))


def _colpack(v):
    v = np.asarray(v, np.float32).reshape(-1)
    return v.reshape(-1, 128).T


def kernel(x_prompt, x_sample, mem_prompt, mem_sample,
           ffn1_norm_pre, ffn1_w_up, ffn1_w_down, ffn1_norm_post,
           mix_norm_pre, mem_norm, w_in, w_mem_kv, w_pool, pool_scale,
           w_br_attn, w_br_pool, w_br_mem, w_gate, b_gate, w_out, mix_norm_post,
           ffn2_norm_pre, ffn2_w_up, ffn2_w_down, ffn2_norm_post, final_norm):
    f32 = lambda a: np.ascontiguousarray(np.asarray(a, dtype=np.float32))
    x_prompt, x_sample, mem_prompt, mem_sample = map(f32, (x_prompt, x_sample, mem_prompt, mem_sample))
    vecs = np.ascontiguousarray(np.concatenate([
        _colpack(ffn1_norm_pre), _colpack(ffn1_norm_post), _colpack(mix_norm_pre), _colpack(mem_norm),
        _colpack(pool_scale), _colpack(b_gate), _colpack(mix_norm_post), _colpack(ffn2_norm_pre),
        _colpack(ffn2_norm_post), _colpack(final_norm)], axis=1).astype(np.float32))
    assert vecs.shape == (128, NV)
    shared = {
        "vecs": vecs, "abias": _consts(),
        "ffn1_w_up": f32(ffn1_w_up).reshape(D, 2 * DFF), "ffn1_w_down": f32(ffn1_w_down).reshape(DFF, D),
        "w_in": f32(w_in).reshape(D, 6656), "w_mem_kv": f32(w_mem_kv).reshape(D, 2048),
        "w_pool": f32(w_pool).reshape(1024, 256), "w_br_attn": f32(w_br_attn).reshape(512, D),
        "w_br_pool": f32(w_br_pool).reshape(1024, D), "w_br_mem": f32(w_br_mem).reshape(1024, D),
        "w_gate": f32(w_gate).reshape(D, 3 * D), "w_out": f32(w_out).reshape(D, D),
        "ffn2_w_up": f32(ffn2_w_up).reshape(D, 2 * DFF), "ffn2_w_down": f32(ffn2_w_down).reshape(DFF, D),
    }
    P_CORE = (0, 1, 4, 5)
    S_CORE = (2, 3, 6, 7)
    in_maps = [None] * 8
    for b in range(4):
        xp = np.zeros((T, D), np.float32)
        xp[:2048] = x_prompt[b]
        m = dict(shared)
        m.update({"x": xp, "mem": mem_prompt[b], "flag": np.zeros((128, 1), np.float32), "pedge": _pedge(2048)})
        in_maps[P_CORE[b]] = m
    for b in range(4):
        m = dict(shared)
        m.update({"x": x_sample[b], "mem": mem_sample[b], "flag": np.ones((128, 1), np.float32), "pedge": _pedge(None)})
        in_maps[S_CORE[b]] = m
    nc, stack = build_program()
    res = run_bass_kernel_spmd(nc, in_maps, core_ids=list(range(8)))
    y_prompt = np.stack([np.asarray(res.results[P_CORE[b]]["y"], np.float32)[:2048] for b in range(4)], axis=0)
    y_sample = np.stack([np.asarray(res.results[S_CORE[b]]["y"], np.float32) for b in range(4)], axis=0)
    return (y_prompt, y_sample)
```

```python
import contextlib
import numpy as np
import concourse.bass as bass
import concourse.mybir as mybir
from concourse.bass_utils import run_bass_kernel_spmd

F32 = mybir.dt.float32
BF16 = mybir.dt.bfloat16
AF = mybir.ActivationFunctionType
ALU = mybir.AluOpType

D = 2048
DC = 16
T = 4096
NT = 512
NTILES = T // NT
DFF = 5504
FC = DFF // 128
NMEM = 256
EPS = 1e-6
NB = 4
SLABW = 256
DILS = (1, 4, 16)
POOLW = (2, 4, 8, 16)

V_F1PRE, V_F1POST, V_MIXPRE, V_MEMN, V_PSCALE, V_BGATE, V_MIXPOST, V_F2PRE, V_F2POST, V_FINAL = (
    0, 16, 32, 48, 64, 72, 120, 136, 152, 168)
NV = 184

C_Q, C_K, C_V, C_U, C_QM = 0, 1536, 3072, 4608, 5632


class Prog:
    ENG = ("pe", "act", "dve", "pool", "sp")

    def __init__(self, nc, stack):
        self.nc = nc
        self.stack = stack
        self.lists = {e: [] for e in self.ENG}
        self.cnt = {e: 0 for e in self.ENG}
        self.csem = {e: stack.enter_context(nc.semaphore("c_" + e)) for e in ("pe", "act", "dve", "pool")}
        self.waited = {e: {} for e in self.ENG}
        self.dsems = {}
        self.last = {e: None for e in self.ENG}

    def wait(self, eng, tok):
        if tok is None:
            return
        if isinstance(tok, list):
            for t in tok:
                self.wait(eng, t)
            return
        sem, val, key = tok
        if self.waited[eng].get(key, 0) >= val:
            return
        self.waited[eng][key] = val
        self.lists[eng].append(("w", sem, val))

    def op(self, eng, fn, waits=(), sig=True):
        for t in waits:
            self.wait(eng, t)
        tok = None
        if sig:
            self.cnt[eng] += 1
            tok = (self.csem[eng], self.cnt[eng], "c_" + eng)
            self.last[eng] = tok
        self.lists[eng].append(("o", fn, sig))
        return tok

    def dma(self, q, out, in_, semname, waits=()):
        for t in waits:
            self.wait(q, t)
        if semname not in self.dsems:
            self.dsems[semname] = [self.stack.enter_context(self.nc.semaphore("d_" + semname)), 0]
        s = self.dsems[semname]
        s[1] += 16
        self.lists[q].append(("d", out, in_, s[0]))
        return (s[0], s[1], "d_" + semname)

    def bar(self):
        return [self.last[e] for e in ("pe", "act", "dve", "pool") if self.last[e] is not None]

    def emit(self):
        nc = self.nc
        with nc.Block() as block:
            def replay(eng_obj, name):
                for it in self.lists[name]:
                    if it[0] == "w":
                        eng_obj.wait_ge(it[1], it[2])
                    elif it[0] == "o":
                        ins = it[1](eng_obj)
                        if it[2]:
                            ins.then_inc(self.csem[name], 1)
                    else:
                        eng_obj.dma_start(out=it[1], in_=it[2]).then_inc(it[3], 16)

            @block.tensor
            def _(e):
                replay(e, "pe")

            @block.scalar
            def _(e):
                replay(e, "act")

            @block.vector
            def _(e):
                replay(e, "dve")

            @block.gpsimd
            def _(e):
                replay(e, "pool")

            @block.sync
            def _(e):
                replay(e, "sp")


class K:
    pass


def build_program(stop=None, dbg=False):
    nc = bass.Bass("TRN2", target_bir_lowering=False)
    stack = contextlib.ExitStack()
    k = K()
    k.nc = nc
    p = Prog(nc, stack)
    k.p = p

    def din(name, shape, dt=F32):
        return nc.dram_tensor(name, shape, dt, kind="ExternalInput").ap()

    x_d = din("x", [T, D])
    mem_d = din("mem", [NMEM, D])
    vecs_d = din("vecs", [128, NV])
    abias_d = din("abias", [128, 12 * 256])
    flag_d = din("flag", [128, 1])
    pedge_d = din("pedge", [128, 4 * 3 * 8])
    W = {}
    for name, shp in (("ffn1_w_up", [D, 2 * DFF]), ("ffn1_w_down", [DFF, D]), ("w_in", [D, 6656]),
                      ("w_mem_kv", [D, 2048]), ("w_pool", [1024, 256]), ("w_br_attn", [512, D]),
                      ("w_br_pool", [1024, D]), ("w_br_mem", [1024, D]), ("w_gate", [D, 3 * D]),
                      ("w_out", [D, D]), ("ffn2_w_up", [D, 2 * DFF]), ("ffn2_w_down", [DFF, D])):
        W[name] = din(name, shp)
    y_d = nc.dram_tensor("y", [T, D], F32, kind="ExternalOutput").ap()

    def dscr(name, shape, dt):
        if dbg:
            return nc.dram_tensor(name, shape, dt, kind="ExternalOutput").ap()
        return nc.dram_tensor(name, shape, dt).ap()

    qkT = dscr("s_qkT", [3072, T], BF16)
    vtok = dscr("s_vtok", [T, 1536], BF16)
    upT = dscr("s_upT", [1024, T], F32)
    qmT = dscr("s_qmT", [1024, T], BF16)
    x1T = dscr("s_x1T", [D, T], F32)
    yaT = dscr("s_yaT", [512, T], BF16)
    if dbg:
        d_kmem = dscr("d_kmem", [128, 8 * NMEM], BF16)
        d_vmem = dscr("d_vmem", [128, 2 * 1024], BF16)

    NJ_PA, NJ_PB = 94, 124
    wcache = nc.dram_tensor("s_wcache", [NJ_PA + NJ_PB, 128, 16 * SLABW], BF16).ap()

    def finalize():
        for name, s_ in p.dsems.items():
            p.wait("sp", (s_[0], s_[1], "d_" + name))
        p.wait("sp", p.bar())
        p.emit()
        return nc, stack

    def sb(name, shape, dt):
        return stack.enter_context(nc.sbuf_tensor("sb_" + name, shape, dt))

    slabs = [sb(f"slab{i}", [128, 16, SLABW], BF16) for i in range(NB)]
    tmpg = [sb(f"tmpg{i}", [128, 2, NT], F32) for i in range(2)]
    sq = [sb(f"sq{i}", [128, NT], BF16) for i in range(4)]
    rstd = [sb(f"rstd{i}", [128, NT], F32) for i in range(2)]
    stage16 = [sb(f"st16_{i}", [128, 2, NT], BF16) for i in range(2)]
    ident = sb("ident", [128, 128], F32)
    ones32 = sb("ones32", [128, 128], F32)
    ones16 = sb("ones16", [128, 128], BF16)
    vecs = sb("vecs", [128, NV], F32)
    wtab = sb("wtab", [128, 12, 256], F32)
    flag = sb("flag", [128, 1], F32)
    pedge = sb("pedge", [128, 4, 3, 8], F32)
    kmemT = sb("kmemT", [128, 8, NMEM], BF16)
    vmem = sb("vmem", [128, 2, 1024], BF16)
    wpool = sb("wpool", [128, 8, 256], BF16)
    ARENA = 125952
    arena = sb("arena", [128, ARENA // 4], F32)
    ps = stack.enter_context(nc.psum_tensor("ps", [128, 8, 512], F32))

    def carve(off, nbytes, dt, pattern=None, **kw):
        a = arena[:, off // 4:(off + nbytes) // 4]
        if dt != F32:
            a = a.bitcast(dt)
        if pattern:
            a = a.rearrange(pattern, **kw)
        return a

    O_XRES, O_HBF, O_HID, O_YBUF = 0, 32768, 49152, 93184
    xres = carve(O_XRES, 32768, F32, "p (c n) -> p c n", n=NT)
    hbf = carve(O_HBF, 16384, BF16, "p (c n) -> p c n", n=NT)
    hid = carve(O_HID, 44032, BF16, "p (c n) -> p c n", n=NT)
    ybuf = carve(O_YBUF, 32768, F32, "p (c n) -> p c n", n=NT)
    xin = carve(O_YBUF, 32768, F32, "p (b f) -> p b f", f=D)
    fin = carve(O_HID, 32768, F32, "p (c n) -> p c n", n=NT)

    k.slab_i = 0
    k.cache_mode = None
    k.cache_next = 0
    k.wb_q = []
    k.slab_wb = [None] * NB
    k.wb_tok = {}
    k.slabfree = [None] * NB
    k.grp_i = 0
    k.cur_grp = 0
    k.deferred = []
    k.brd = {b: [] for b in range(8)}
    k.sq_i = 0
    k.sq_free = [None] * 4
    k.rstd_i = 0
    k.tmpg_i = 0
    k.tmpg_free = [[], []]
    k.st16_i = 0
    k.st16_free = [None, None]

    def vcol(base, c):
        return vecs[:, base + c:base + c + 1]

    def flush_deferred():
        d = k.deferred
        k.deferred = []
        for fn in d:
            fn()

    def writeback_pending(keep=0):
        while len(k.wb_q) > keep:
            b, ld, cid, nk, ncols = k.wb_q.pop(0)
            t = p.dma("pool", wcache[cid, :, 0:nk * ncols].rearrange("p (kc n) -> p kc n", n=ncols),
                      slabs[b][:, 0:nk, 0:ncols], f"wb{b}", waits=[ld])
            k.slab_wb[b] = t
            k.wb_tok[cid] = t

    def issue_slab(j):
        b = k.slab_i % NB
        k.slab_i += 1
        nk, ncols, k0, c0 = j["nk"], j["ncols"], j["k0"], j["c0"]
        cid = None
        if k.cache_mode is not None:
            cid = k.cache_next
            k.cache_next += 1
        if k.cache_mode == "use":
            src = wcache[cid, :, 0:nk * ncols].rearrange("p (kc n) -> p kc n", n=ncols)
            ld = p.dma("pool", slabs[b][:, 0:nk, 0:ncols], src, f"slab{b}",
                       waits=[k.slabfree[b], k.slab_wb[b], k.wb_tok[cid]])
            return b, ld
        src = j["w"][k0 * 128:(k0 + nk) * 128, c0:c0 + ncols].rearrange("(kc p) n -> p kc n", p=128)
        ld = p.dma("pool", slabs[b][:, 0:nk, 0:ncols], src, f"slab{b}", waits=[k.slabfree[b], k.slab_wb[b]])
        if k.cache_mode == "fill":
            writeback_pending(keep=1)
            k.wb_q.append((b, ld, cid, nk, ncols))
        return b, ld

    def prefetch(jobs):
        for j in jobs[:NB]:
            j["pf"] = issue_slab(j)

    def do_job(j):
        b, ld = j["pf"] if "pf" in j else issue_slab(j)
        slab = slabs[b]
        nk, ncols, k0, c0 = j["nk"], j["ncols"], j["k0"], j["c0"]
        if j["start"]:
            g = k.grp_i % 3
            k.grp_i += 1
            k.cur_grp = g
            w_pe = [ld] + list(k.brd[2 * g]) + list(k.brd[2 * g + 1]) + list(j.get("waits", []))
        else:
            g = k.cur_grp
            w_pe = [ld] + list(j.get("waits", []))
        banks = (2 * g, 2 * g + 1)
        N = j["n"]
        act = j["act"]
        tok = None
        ops = []
        if j["mode"] == "A":
            nch = ncols // 128
            for i in range(nch):
                for kc in range(nk):
                    ops.append((ps[:, banks[i], 0:N], slab[:, kc, i * 128:(i + 1) * 128], act(k0 + kc),
                                j["start"] and kc == 0, j["stop"] and kc == nk - 1))
        else:
            ntb = N // 128
            for tb in range(ntb):
                for kc in range(nk):
                    ops.append((ps[:, banks[tb], 0:ncols] if ntb == 2 else ps[:, banks[tb // 2], (tb % 2) * 256:(tb % 2) * 256 + ncols],
                                act(k0 + kc)[:, tb * 128:(tb + 1) * 128], slab[:, kc, 0:ncols],
                                j["start"] and kc == 0, j["stop"] and kc == nk - 1))
        for idx, (o, l, r, st, sp_) in enumerate(ops):
            tok = p.op("pe", lambda e, o=o, l=l, r=r, st=st, sp_=sp_: e.matmul(o, l, r, start=st, stop=sp_),
                       waits=w_pe if idx == 0 else (), sig=(idx == len(ops) - 1))
        k.slabfree[b] = tok
        flush_deferred()
        if j["stop"]:
            readers = j["evac"](banks, tok)
            k.brd[banks[0]] = list(readers)
            k.brd[banks[1]] = list(readers)

    def mk_jobs(w, K_chunks, col_blocks, act, n, mode, evac_for, waits=()):
        jobs = []
        kgs = [(s, min(16, K_chunks - s)) for s in range(0, K_chunks, 16)]
        first = True
        for bi, (c0, ncols) in enumerate(col_blocks):
            for gi, (k0, nk) in enumerate(kgs):
                jobs.append(dict(w=w, k0=k0, nk=nk, c0=c0, ncols=ncols, act=act, n=n, mode=mode,
                                 start=(gi == 0), stop=(gi == len(kgs) - 1), evac=evac_for(bi),
                                 waits=list(waits) if first else []))
                first = False
        return jobs

    def stats_add(src_ap, N, first, last, src_waits, on="act"):
        i = k.sq_i % 4
        k.sq_i += 1
        sqt = sq[i]
        if on == "act":
            t_sq = p.op("act", lambda e: e.activation(out=sqt[:, 0:N], in_=src_ap, func=AF.Square),
                        waits=list(src_waits) + [k.sq_free[i]])
        else:
            t_sq = p.op("dve", lambda e: e.tensor_tensor(out=sqt[:, 0:N], in0=src_ap, in1=src_ap, op=ALU.mult),
                        waits=list(src_waits) + [k.sq_free[i]])

        def pe_part():
            w = [t_sq] + (list(k.brd[6]) if first else [])
            t = p.op("pe", lambda e: e.matmul(ps[:, 6, 0:N], ones16[:, :], sqt[:, 0:N], start=first, stop=last),
                     waits=w)
            k.sq_free[i] = t
            k.stats_tok = t
        return t_sq, pe_part

    def compute_rstd(N):
        i = k.rstd_i % 2
        k.rstd_i += 1
        r = rstd[i]
        t1 = p.op("act", lambda e: e.activation(out=r[:, 0:N], in_=ps[:, 6, 0:N], func=AF.Sqrt, bias=eps_col[:, 0:1],
                                                scale=1.0 / D),
                  waits=[k.stats_tok] + p.bar())
        k.brd[6] = [t1]
        t2 = p.op("dve", lambda e: e.reciprocal(out=r[:, 0:N], in_=r[:, 0:N]), waits=[t1])
        return r, t2

    def prenorm(src, gbase, dst, N, src_waits, nchunks=DC, cw=None):
        for c in range(nchunks):
            _, pe_part = stats_add(src[:, c, 0:N], N, c == 0, c == nchunks - 1,
                                   [cw[c]] if cw is not None else src_waits, on="act")
            pe_part()
        r, t_r = compute_rstd(N)
        toks = []
        for c in range(nchunks):
            toks.append(p.op("dve", lambda e, c=c: e.scalar_tensor_tensor(
                out=dst[:, c, 0:N], in0=src[:, c, 0:N], scalar=vcol(gbase, c), in1=r[:, 0:N],
                op0=ALU.mult, op1=ALU.mult), waits=[t_r] + list(src_waits)))
        return toks[-1]

    def out_evac_for(nchunk_total):
        def evac_for(bi):
            def evac(banks, pe_tok):
                readers = []
                defs = []
                for i in range(2):
                    c = bi * 2 + i
                    t_cp = p.op("dve", lambda e, c=c, i=i: e.tensor_copy(ybuf[:, c, :], ps[:, banks[i], :]),
                                waits=[pe_tok])
                    t_sq, pe_part = stats_add(ybuf[:, c, :], NT, c == 0, c == nchunk_total - 1, [t_cp], on="act")
                    readers += [t_cp]
                    defs.append(pe_part)
                k.deferred += defs
                return readers
            return evac
        return evac_for

    def post_residual(gbase):
        flush_deferred()
        r, t_r = compute_rstd(NT)
        tl = None
        k.res_cw = []
        for c in range(DC):
            t1 = p.op("pool" if c % 3 != 2 else "dve",
                      lambda e, c=c: e.tensor_tensor(out=ybuf[:, c, :], in0=ybuf[:, c, :], in1=r[:, :], op=ALU.mult),
                      waits=[t_r])
            tl = p.op("dve", lambda e, c=c: e.scalar_tensor_tensor(
                out=xres[:, c, :], in0=ybuf[:, c, :], scalar=vcol(gbase, c), in1=xres[:, c, :],
                op0=ALU.mult, op1=ALU.add), waits=[t1])
            k.res_cw.append(tl)
        return tl

    def ffn(w_up, w_down, g_pre, g_post, x_waits, cw=None):
        t_h = prenorm(xres, g_pre, hbf, NT, x_waits, cw=cw)
        pairs = [(pi * 256, min(256, DFF - pi * 256)) for pi in range((DFF + 255) // 256)]

        def evac_g_for(bi):
            def evac(banks, pe_tok):
                i = k.tmpg_i % 2
                k.tmpg_i += 1
                k.cur_tmpg = i
                nch = pairs[bi][1] // 128
                toks = []
                for ch in range(nch):
                    toks.append(p.op("act", lambda e, ch=ch, i=i: e.activation(out=tmpg[i][:, ch, :], in_=ps[:, banks[ch], :],
                                                                               func=AF.Silu),
                                     waits=[pe_tok] + list(k.tmpg_free[i])))
                k.tmpg_tok = toks[-1]
                return toks
            return evac

        def evac_u_for(bi):
            def evac(banks, pe_tok):
                i = k.cur_tmpg
                nch = pairs[bi][1] // 128
                toks = []
                for ch in range(nch):
                    c = bi * 2 + ch
                    toks.append(p.op("dve", lambda e, ch=ch, i=i, c=c: e.tensor_tensor(
                        out=hid[:, c, :], in0=tmpg[i][:, ch, :], in1=ps[:, banks[ch], :], op=ALU.mult),
                        waits=[pe_tok, k.tmpg_tok]))
                k.tmpg_free[i] = [toks[-1]]
                k.hid_tok = toks[-1]
                return toks
            return evac

        first = True
        for bi, (c0, ncols) in enumerate(pairs):
            for which in (0, 1):
                do_job(dict(w=w_up, k0=0, nk=16, c0=c0 + which * DFF, ncols=ncols, act=lambda kc: hbf[:, kc, :], n=NT,
                            mode="A", start=True, stop=True, evac=(evac_g_for if which == 0 else evac_u_for)(bi),
                            waits=[t_h] if first else []))
                first = False
        jobs = mk_jobs(w_down, FC, [(cb * 256, 256) for cb in range(8)], lambda kc: hid[:, kc, :], NT, "A",
                       out_evac_for(DC), waits=[k.hid_tok])
        for j in jobs:
            do_job(j)
        return post_residual(g_post)

    eps_col = sb("eps_col", [128, 1], F32)
    t_c = []
    t_c.append(p.dma("sp", vecs[:, :], vecs_d, "c0"))
    t_c.append(p.dma("sp", wtab[:, :, :], abias_d.rearrange("p (g n) -> p g n", n=256), "c0"))
    t_c.append(p.dma("sp", flag[:, :], flag_d, "c0"))
    t_c.append(p.dma("sp", pedge[:, :, :, :], pedge_d.rearrange("p (g r n) -> p g r n", g=4, r=3), "c0"))
    t_wp = p.dma("pool", wpool[:, :, :], W["w_pool"].rearrange("(gi p) n -> p gi n", p=128), "c1")
    p.op("pool", lambda e: e.memset(ident[:, :], 0.0))
    p.op("pool", lambda e: e.affine_select(out=ident[:, :], in_=ident[:, :], compare_op=ALU.not_equal, fill=1.0,
                                           base=0, pattern=[[-1, 128]], channel_multiplier=1), waits=[p.last["pool"]])
    p.op("pool", lambda e: e.memset(ones32[:, :], 1.0))
    p.op("pool", lambda e: e.memset(ones16[:, :], 1.0))
    p.op("pool", lambda e: e.memset(eps_col[:, :], EPS))
    t_const = p.last["pool"]
    t_w = p.op("dve", lambda e: e.tensor_scalar(out=wtab[:, :, :], in0=wtab[:, :, :], scalar1=float(128 ** 0.5),
                                                scalar2=None, op0=ALU.mult), waits=[t_c[-1]])
    t_v = p.op("dve", lambda e: e.tensor_scalar(out=vecs[:, V_F1POST:V_F1POST + 16], in0=vecs[:, V_F1POST:V_F1POST + 16],
                                                scalar1=0.5, scalar2=None, op0=ALU.mult), waits=[t_c[-1]])
    t_v = p.op("dve", lambda e: e.tensor_scalar(out=vecs[:, V_F2POST:V_F2POST + 16], in0=vecs[:, V_F2POST:V_F2POST + 16],
                                                scalar1=0.5, scalar2=None, op0=ALU.mult), waits=[t_v])
    setup_toks = [t_const, t_w, t_v, t_wp]
    k.stats_tok = None
    if stop == "P0":
        return finalize()

    memin = carve(O_YBUF, 16384, F32, "p (b f) -> p b f", f=D)
    memT = carve(O_XRES, 16384, F32, "p (c n) -> p c n", n=NMEM)
    memh = carve(O_HBF, 8192, BF16, "p (c n) -> p c n", n=NMEM)
    t_ld = p.dma("sp", memin[:, :, :], mem_d.rearrange("(b p) f -> p b f", p=128), "xin")
    tr_toks = []
    for c in range(DC):
        bank = 6 + (c % 2)
        tp = None
        for mb in range(2):
            tp = p.op("pe", lambda e, c=c, mb=mb, bank=bank: e.transpose(ps[:, bank, mb * 128:(mb + 1) * 128],
                                                                         memin[:, mb, c * 128:(c + 1) * 128], ident[:, :]),
                      waits=[t_ld] + setup_toks + list(k.brd[bank]), sig=(mb == 1))
        eng = "act" if c % 2 else "dve"
        if eng == "act":
            tcp = p.op("act", lambda e, c=c, bank=bank: e.copy(out=memT[:, c, :], in_=ps[:, bank, 0:NMEM]), waits=[tp])
        else:
            tcp = p.op("dve", lambda e, c=c, bank=bank: e.tensor_copy(memT[:, c, :], ps[:, bank, 0:NMEM]), waits=[tp])
        k.brd[bank] = [tcp]
        tr_toks.append(tcp)
    if stop == "PMT":
        return finalize()
    t_mh = prenorm(memT, V_MEMN, memh, NMEM, tr_toks[-2:])
    if stop == "PMN":
        return finalize()

    def evac_kmem_for(bi):
        def evac(banks, pe_tok):
            toks = []
            for i in range(2):
                toks.append(p.op("act" if i else "dve",
                                 (lambda e, i=i: e.copy(out=kmemT[:, bi * 2 + i, :], in_=ps[:, banks[i], 0:NMEM])) if i else
                                 (lambda e, i=i: e.tensor_copy(kmemT[:, bi * 2 + i, :], ps[:, banks[i], 0:NMEM])),
                                 waits=[pe_tok]))
            return toks
        return evac

    def evac_vmem_for(bi):
        def evac(banks, pe_tok):
            toks = []
            for mb in range(2):
                src = ps[:, banks[mb], 0:256]
                dst = vmem[:, mb, bi * 256:(bi + 1) * 256]
                toks.append(p.op("act" if mb else "dve",
                                 (lambda e, s=src, d_=dst: e.copy(out=d_, in_=s)) if mb else
                                 (lambda e, s=src, d_=dst: e.tensor_copy(d_, s)), waits=[pe_tok]))
            return toks
        return evac

    for j in mk_jobs(W["w_mem_kv"], 16, [(cb * 256, 256) for cb in range(4)], lambda kc: memh[:, kc, :], NMEM, "A",
                     evac_kmem_for, waits=[t_mh]):
        do_job(j)
    if stop == "PMK":
        flush_deferred()
        return finalize()
    if stop == "PMKK":
        for j in mk_jobs(W["w_mem_kv"], 16, [(cb * 256, 256) for cb in range(4)], lambda kc: memh[:, kc, :], NMEM, "A",
                         evac_kmem_for):
            do_job(j)
        flush_deferred()
        return finalize()
    for j in mk_jobs(W["w_mem_kv"], 16, [(1024 + cb * 256, 256) for cb in range(4)], lambda kc: memh[:, kc, :], NMEM, "B",
                     evac_vmem_for):
        do_job(j)
    flush_deferred()

    if dbg:
        p.dma("sp", d_kmem, kmemT[:, :, :].rearrange("p a b -> p (a b)"), "dbg", waits=p.bar())
        p.dma("sp", d_vmem, vmem[:, :, :].rearrange("p a b -> p (a b)"), "dbg", waits=p.bar())
    if stop == "PM":
        return finalize()
    scr_w = {"qkT": [], "vtok": [], "upT": [], "qmT": [], "x1T": [], "yaT": []}
    xin_free = p.bar()
    xres_free = p.bar()
    def pa_load(ti_, free_w):
        t0_ = ti_ * NT
        return p.dma("sp", xin[:, :, :], x_d[t0_:t0_ + NT, :].rearrange("(b p) f -> p b f", p=128), "xin", waits=free_w)

    def pa_transposes(t_ld, xres_free):
        tr = []
        for c in range(DC):
            bank = 6 + (c % 2)
            tp = None
            for tb in range(4):
                tp = p.op("pe", lambda e, c=c, tb=tb, bank=bank: e.transpose(ps[:, bank, tb * 128:(tb + 1) * 128],
                                                                             xin[:, tb, c * 128:(c + 1) * 128], ident[:, :]),
                          waits=[t_ld] + list(k.brd[bank]), sig=(tb == 3))
            if c % 2:
                tcp = p.op("act", lambda e, c=c, bank=bank: e.copy(out=xres[:, c, :], in_=ps[:, bank, :]),
                           waits=[tp] + xres_free)
            else:
                tcp = p.op("dve", lambda e, c=c, bank=bank: e.tensor_copy(xres[:, c, :], ps[:, bank, :]),
                           waits=[tp] + xres_free)
            k.brd[bank] = [tcp]
            tr.append(tcp)
        return tr

    tr_toks = pa_transposes(pa_load(0, xin_free), xres_free)
    for ti in range(NTILES):
        t0 = ti * NT
        writeback_pending()
        k.cache_mode = "fill" if ti == 0 else "use"
        k.cache_next = 0
        t_x1 = ffn(W["ffn1_w_up"], W["ffn1_w_down"], V_F1PRE, V_F1POST, tr_toks[-2:])
        t_sp = p.dma("sp", x1T.rearrange("(c p) t -> p c t", p=128)[:, :, t0:t0 + NT], xres[:, :, :], "x1T", waits=[t_x1])
        scr_w["x1T"] = [t_sp]
        t_h = prenorm(xres, V_MIXPRE, hbf, NT, [t_x1], cw=list(k.res_cw))
        xin_free = p.bar()
        xres_free = [t_sp] + p.bar()
        t_ld_next = pa_load(ti + 1, xin_free) if ti + 1 < NTILES else None

        def evac_fm16_for(dst_scr, row0, key):
            def evac_for(bi):
                def evac(banks, pe_tok):
                    i = k.st16_i % 2
                    k.st16_i += 1
                    st = stage16[i]
                    t_a = p.op("act", lambda e: e.copy(out=st[:, 0, :], in_=ps[:, banks[0], :]),
                               waits=[pe_tok, k.st16_free[i]])
                    t_b = p.op("dve", lambda e: e.tensor_copy(st[:, 1, :], ps[:, banks[1], :]),
                               waits=[pe_tok, k.st16_free[i]])
                    r0 = row0 + bi * 256
                    t_d = p.dma("sp", dst_scr[r0:r0 + 256, t0:t0 + NT].rearrange("(c p) t -> p c t", p=128), st[:, :, :],
                                f"st16_{i}", waits=[t_a, t_b])
                    k.st16_free[i] = t_d
                    scr_w[key] = [kk for kk in scr_w[key] if kk[2] != t_d[2]] + [t_d]
                    return [t_a, t_b]
                return evac
            return evac_for

        def evac_v_for(bi):
            def evac(banks, pe_tok):
                i = k.st16_i % 2
                k.st16_i += 1
                st = stage16[i].rearrange("p a (b c) -> p (a b) c", c=256)
                toks = []
                for tb in range(4):
                    src = ps[:, banks[tb // 2], (tb % 2) * 256:(tb % 2) * 256 + 256]
                    if tb // 2:
                        toks.append(p.op("act", lambda e, s=src, tb=tb: e.copy(out=st[:, tb, :], in_=s),
                                         waits=[pe_tok, k.st16_free[i]]))
                    else:
                        toks.append(p.op("dve", lambda e, s=src, tb=tb: e.tensor_copy(st[:, tb, :], s),
                                         waits=[pe_tok, k.st16_free[i]]))
                t_d = p.dma("sp", vtok[t0:t0 + NT, bi * 256:(bi + 1) * 256].rearrange("(b p) c -> p b c", p=128), st,
                            f"st16_{i}", waits=toks)
                k.st16_free[i] = t_d
                scr_w["vtok"] = [kk for kk in scr_w["vtok"] if kk[2] != t_d[2]] + [t_d]
                return toks
            return evac

        def evac_u_for(bi):
            def evac(banks, pe_tok):
                i = k.tmpg_i % 2
                k.tmpg_i += 1
                st = tmpg[i]
                t_a = p.op("act", lambda e: e.copy(out=st[:, 0, :], in_=ps[:, banks[0], :]),
                           waits=[pe_tok] + list(k.tmpg_free[i]))
                t_b = p.op("dve", lambda e: e.tensor_copy(st[:, 1, :], ps[:, banks[1], :]),
                           waits=[pe_tok] + list(k.tmpg_free[i]))
                r0 = bi * 256
                t_d = p.dma("sp", upT[r0:r0 + 256, t0:t0 + NT].rearrange("(c p) t -> p c t", p=128), st[:, :, :],
                            f"tmpg_{i}", waits=[t_a, t_b])
                k.tmpg_free[i] = [t_d]
                scr_w["upT"] = [kk for kk in scr_w["upT"] if kk[2] != t_d[2]] + [t_d]
                return [t_a, t_b]
            return evac

        hact = lambda kc: hbf[:, kc, :]
        for j in mk_jobs(W["w_in"], 16, [(C_Q + cb * 256, 256) for cb in range(12)], hact, NT, "A",
                         evac_fm16_for(qkT, 0, "qkT"), waits=[t_h]):
            do_job(j)
        if t_ld_next is not None:
            tr_toks = pa_transposes(t_ld_next, xres_free)
        for j in mk_jobs(W["w_in"], 16, [(C_V + cb * 256, 256) for cb in range(6)], hact, NT, "B", evac_v_for):
            do_job(j)
        for j in mk_jobs(W["w_in"], 16, [(C_U + cb * 256, 256) for cb in range(4)], hact, NT, "A", evac_u_for):
            do_job(j)
        for j in mk_jobs(W["w_in"], 16, [(C_QM + cb * 256, 256) for cb in range(4)], hact, NT, "A",
                         evac_fm16_for(qmT, 0, "qmT")):
            do_job(j)
        flush_deferred()
        if stop == "PA1":
            writeback_pending()
            return finalize()

    writeback_pending()
    assert k.cache_next == NJ_PA, k.cache_next
    k.cache_mode = None
    if stop == "PA":
        return finalize()
    flush_deferred()
    QT = [carve(0 + i * 8192, 8192, BF16) for i in range(2)]
    KT = [carve(16384 + i * 8192, 8192, BF16) for i in range(2)]
    VT = [carve(32768 + i * 8192, 8192, BF16, "p (c n) -> p c n", n=128) for i in range(2)]
    num = carve(49152, 16384, F32)
    den = carve(65536, 16384, F32)
    yab = carve(81920, 8192, BF16)
    Eb = [carve(90112 + i * 1024, 1024, F32) for i in range(4)]
    Pb = [carve(94208 + i * 512, 512, BF16) for i in range(4)]
    att_start = p.bar() + [scr_w["x1T"][0]]
    qkv_free = [att_start, att_start]
    blk_i = 0
    s_rd = [[], [], [], []]
    o_rd = [[], [], [], []]
    e_free = [None] * 4
    p_free = [None] * 4
    gh_i = 0
    scale = 128 ** -0.5
    t_yab = None
    for hh in range(4):
        t_z1 = p.op("pool", lambda e: e.memset(num[:, :], 0.0), waits=att_start + ([t_yab] if t_yab else []))
        t_z2 = p.op("pool", lambda e: e.memset(den[:, :], 0.0), waits=att_start + ([t_yab] if t_yab else []))
        acc_tok = [t_z1, t_z2]
        for g in range(3):
            d = DILS[g]
            L = T // d
            nch = L // 128
            gi = g * 4 + hh
            bsel = gh_i % 2
            gh_i += 1
            row = g * 512 + hh * 128
            wq = list(qkv_free[bsel]) + scr_w["qkT"] + scr_w["vtok"]
            lq = p.dma("sp", QT[bsel][:, :], qkT[row:row + 128, :], f"qkv{bsel}", waits=wq)
            lk = p.dma("sp", KT[bsel][:, :], qkT[1536 + row:1536 + row + 128, :], f"qkv{bsel}")
            lv = None
            vsrc = vtok[:, row:row + 128].rearrange("(jk p r) c -> r p jk c", p=128, r=d)
            for r in range(d):
                for j0 in range(0, nch, 8):
                    j1 = min(nch, j0 + 8)
                    lv = p.dma("sp", VT[bsel][:, r * nch + j0:r * nch + j1, :], vsrc[r][:, j0:j1, :], f"qkv{bsel}")
            ld_tok = lv
            jb = (2048 // d) // 128
            pend = None
            blocks = [(r, j) for r in range(d) for j in range(nch + 1)]

            def emit_s(r, j, slot):
                qa = 64 if j == 0 else 0
                qb = 64 if j == nch else 128
                qs = r + d * (128 * j - 64 + qa)
                nq = qb - qa
                qap = QT[bsel][:, qs:qs + d * (nq - 1) + 1:d]
                hv = [(half, jk) for half, jk in ((0, j - 1), (1, j)) if 0 <= jk < nch]
                c0_ = hv[0][0] * 128 + qa
                c1_ = hv[-1][0] * 128 + qb
                p.op("pe", lambda e, c0_=c0_, c1_=c1_, gi=gi: e.matmul(ps[:, 2 * slot, c0_:c1_], ident[:, :], wtab[:, gi, c0_:c1_],
                                                                        start=True, stop=False),
                     waits=[ld_tok] + list(s_rd[slot]) + setup_toks, sig=False)
                tk = None
                for n_, (half, jk) in enumerate(hv):
                    ks = r + d * 128 * jk
                    kap = KT[bsel][:, ks:ks + d * 127 + 1:d]
                    o = ps[:, 2 * slot, half * 128 + qa:half * 128 + qb]
                    last = (n_ == len(hv) - 1)
                    tk = p.op("pe", lambda e, o=o, kap=kap, qap=qap, last=last: e.matmul(o, kap, qap, start=False, stop=last),
                              sig=last)
                return tk

            def emit_rest(r, j, slot, t_s):
                qa = 64 if j == 0 else 0
                qb = 64 if j == nch else 128
                E, P_ = Eb[slot], Pb[slot]
                halves = [h for h in (0, 1) if 0 <= j - 1 + h < nch]
                c0_ = halves[0] * 128 + qa
                c1_ = halves[-1] * 128 + qb
                flagged = (j == jb and 1 in halves)
                if not flagged:
                    t_p = p.op("act", lambda e: e.activation(out=P_[:, c0_:c1_], in_=ps[:, 2 * slot, c0_:c1_], func=AF.Exp,
                                                             scale=scale), waits=[t_s, p_free[slot]])
                    s_rd[slot] = [t_p]
                else:
                    t_e = p.op("act", lambda e: e.activation(out=E[:, c0_:c1_], in_=ps[:, 2 * slot, c0_:c1_], func=AF.Exp,
                                                             scale=scale), waits=[t_s, e_free[slot]])
                    s_rd[slot] = [t_e]
                    t_p = None
                    for h in halves:
                        cs = slice(h * 128 + qa, h * 128 + qb)
                        if h == 1:
                            t_p = p.op("dve", lambda e, cs=cs: e.tensor_scalar(
                                out=P_[:, cs], in0=E[:, cs], scalar1=flag[:, 0:1], scalar2=None, op0=ALU.mult),
                                waits=[t_e, p_free[slot]])
                        else:
                            t_p = p.op("dve", lambda e, cs=cs: e.tensor_copy(P_[:, cs], E[:, cs]), waits=[t_e, p_free[slot]])
                    e_free[slot] = t_p
                nq = qb - qa
                t_pv = None
                for which in (0, 1):
                    for n_, h in enumerate(halves):
                        jk = j - 1 + h
                        cs = slice(h * 128 + qa, h * 128 + qb)
                        lhs = VT[bsel][:, r * nch + jk, :] if which == 0 else ones16[:, :]
                        o = ps[:, 2 * slot + 1, which * 128 + qa:which * 128 + qb]
                        t_pv = p.op("pe", lambda e, o=o, lhs=lhs, cs=cs, n_=n_: e.matmul(
                            o, lhs, P_[:, cs], start=(n_ == 0), stop=(n_ == len(halves) - 1)),
                            waits=[t_p] + list(o_rd[slot]), sig=True)
                p_free[slot] = t_pv
                qs = r + d * (128 * j - 64 + qa)
                sl = slice(qs, qs + d * (nq - 1) + 1, d)
                t_n = p.op("dve", lambda e: e.tensor_tensor(out=num[:, sl], in0=num[:, sl],
                                                            in1=ps[:, 2 * slot + 1, qa:qb], op=ALU.add),
                           waits=[t_pv] + acc_tok)
                t_d = p.op("dve", lambda e: e.tensor_tensor(out=den[:, sl], in0=den[:, sl],
                                                            in1=ps[:, 2 * slot + 1, 128 + qa:128 + qb], op=ALU.add),
                           waits=[t_pv] + acc_tok)
                o_rd[slot] = [t_n, t_d]
                return t_d

            last_tok = None
            pendq = []
            for (r, j) in blocks:
                slot = blk_i % 4
                blk_i += 1
                t_s = emit_s(r, j, slot)
                pendq.append((r, j, slot, t_s))
                if len(pendq) > 3:
                    last_tok = emit_rest(*pendq.pop(0))
            while pendq:
                last_tok = emit_rest(*pendq.pop(0))
            acc_tok = [last_tok]
            qkv_free[bsel] = [p.last["pe"], last_tok]
        t_r = p.op("dve", lambda e: e.reciprocal(out=den[:, :], in_=den[:, :]), waits=acc_tok)
        t_y = p.op("dve", lambda e: e.tensor_tensor(out=yab[:, :], in0=num[:, :], in1=den[:, :], op=ALU.mult),
                   waits=[t_r, t_yab])
        t_yab = p.dma("sp", yaT[hh * 128:(hh + 1) * 128, :], yab[:, :], "yab", waits=[t_y])
        att_start = [t_y]
    scr_w["yaT"] = [t_yab]

    if stop == "PATT":
        return finalize()
    H0 = O_HID
    Y0 = O_YBUF
    merged = carve(H0, 32768, F32, "p (c n) -> p c n", n=NT)
    utg = [[carve(H0 + i * 12672 + a * 4224, 4224, F32, "p (c n) -> p c n", n=528) for a in range(3)] for i in range(2)]
    Pm = [carve(H0 + 25344 + i * 2048, 2048, BF16, "p (c n) -> p c n", n=NT) for i in range(2)]
    rden = carve(H0 + 25344 + 4096, 2048, F32)
    ya = carve(H0 + 32768, 4096, BF16, "p (c n) -> p c n", n=NT)
    pooled = [carve(H0 + 36864 + i * 2048, 2048, BF16, "p (c n) -> p c n", n=NT) for i in range(2)]
    ypool = carve(Y0, 8192, BF16, "p (c n) -> p c n", n=NT)
    ymem = carve(Y0 + 8192, 8192, BF16, "p (c n) -> p c n", n=NT)
    qm = carve(Y0 + 16384, 8192, BF16, "p (c n) -> p c n", n=NT)
    xout = xin
    tile_free = p.bar() + [t_yab]
    xres_free2 = list(tile_free)
    fin_free = list(tile_free)
    utg_free = [[], []]
    pooled_free = [[], []]
    pm_free = [[], []]
    out_toks = []
    for ti in range(NTILES):
        t0 = ti * NT
        writeback_pending()
        k.cache_mode = "fill" if ti == 0 else "use"
        k.cache_next = NJ_PA
        wr = scr_w["x1T"] + scr_w["upT"] + scr_w["qmT"] + scr_w["yaT"]
        if ti == 0:
            t_x = p.dma("sp", xres[:, :, :], x1T.rearrange("(c p) t -> p c t", p=128)[:, :, t0:t0 + NT], "ldx",
                        waits=xres_free2 + wr)
        else:
            t_x = t_x_next
        t_ya = p.dma("sp", ya[:, :, :], yaT.rearrange("(c p) t -> p c t", p=128)[:, :, t0:t0 + NT], "ldya", waits=fin_free)
        t_h = prenorm(xres, V_MIXPRE, hbf, NT, [t_x])
        brjobs = []
        for br_, (wbr_, kcb_, ybr_) in enumerate(((W["w_br_attn"], 4, ya), (W["w_br_pool"], 8, ypool), (W["w_br_mem"], 8, ymem))):
            for cb_ in range(8):
                brjobs.append(dict(w=W["w_gate"], k0=0, nk=16, c0=br_ * D + cb_ * 256, ncols=256,
                                   act=lambda kc: hbf[:, kc, :], n=NT, mode="A", start=True, stop=True))
                brjobs.append(dict(w=wbr_, k0=0, nk=kcb_, c0=cb_ * 256, ncols=256,
                                   act=lambda kc, ybr_=ybr_: ybr_[:, kc, :], n=NT, mode="A", start=True, stop=True))
        prefetch(brjobs)
        lo = max(0, t0 - 8)
        hi_ = min(T, t0 + NT + 8)
        for g in range(4):
            w_ = POOLW[g]
            hw = w_ // 2
            bi = g % 2
            U, SA, SB = utg[bi]
            free_w = fin_free + utg_free[bi]
            t_u = p.dma("sp", U[:, :, lo - (t0 - 8):hi_ - (t0 - 8)],
                        upT[g * 256:(g + 1) * 256, lo:hi_].rearrange("(c p) t -> p c t", p=128), f"ldu{bi}", waits=free_w)
            tcur = [t_u]
            if ti == 0:
                tcur.append(p.op("pool", lambda e, U=U: e.memset(U[:, :, 0:8], 0.0), waits=free_w))
            if ti == NTILES - 1:
                tcur.append(p.op("pool", lambda e, U=U: e.memset(U[:, :, 520:528], 0.0), waits=free_w))
            src = U
            width = 528
            bufs = [SA, SB]
            for s_ in range(g + 1):
                sh = 1 << s_
                width -= sh
                dst = bufs[s_ % 2]
                tn = p.op("pool", lambda e, dst=dst, src=src, sh=sh, width=width: e.tensor_tensor(
                    out=dst[:, :, 0:width], in0=src[:, :, 0:width], in1=src[:, :, sh:sh + width], op=ALU.add),
                    waits=tcur + free_w)
                tcur = [tn]
                src = dst
            S = src
            base = 8 - hw
            edges = []
            if ti == 0:
                edges.append((0, 0))
            if ti == 3:
                edges.append((1, NT - 8))
            if ti == NTILES - 1:
                edges.append((2, NT - 8))
            for (reg, col) in edges:
                for ch in range(2):
                    tn = p.op("pool", lambda e, S=S, ch=ch, reg=reg, col=col, g=g, base=base: e.tensor_tensor(
                        out=S[:, ch, base + col:base + col + 8], in0=S[:, ch, base + col:base + col + 8],
                        in1=pedge[:, g, reg, :], op=ALU.mult), waits=tcur)
                    tcur = [tn]
            pb_ = pooled[bi]
            t_pl = p.op("dve", lambda e, S=S, U=U, pb_=pb_, base=base, w_=w_: e.scalar_tensor_tensor(
                out=pb_[:, :, :], in0=S[:, :, base:base + NT], scalar=1.0 / w_, in1=U[:, :, 8:8 + NT],
                op0=ALU.mult, op1=ALU.subtract), waits=tcur + pooled_free[bi] + fin_free)
            utg_free[bi] = [t_pl]
            tm = None
            for oc in range(2):
                bank = 7
                for ic in range(2):
                    tm = p.op("pe", lambda e, oc=oc, ic=ic, g=g, bank=bank, pb_=pb_: e.matmul(
                        ps[:, bank, :], wpool[:, g * 2 + ic, oc * 128:(oc + 1) * 128], pb_[:, ic, :],
                        start=(ic == 0), stop=(ic == 1)),
                        waits=[t_pl] + setup_toks + list(k.brd[bank]), sig=(ic == 1))
                te = p.op("act", lambda e, oc=oc, g=g, bank=bank: e.activation(
                    out=ypool[:, g * 2 + oc, :], in_=ps[:, bank, :], func=AF.Copy,
                    scale=vcol(V_PSCALE, g * 2 + oc)), waits=[tm] + tile_free)
                k.brd[bank] = [te]
            pooled_free[bi] = [tm]
        t_qm = p.dma("sp", qm[:, :, :], qmT.rearrange("(c p) t -> p c t", p=128)[:, :, t0:t0 + NT], "ldq", waits=tile_free)
        for hd in range(4):
            pm = Pm[hd % 2]
            tE = []
            for mc in range(2):
                tS = None
                sbk = (0 if hd % 2 == 0 else 4) + mc
                for cc in range(2):
                    tS = p.op("pe", lambda e, mc=mc, cc=cc, hd=hd, sbk=sbk: e.matmul(
                        ps[:, sbk, :], kmemT[:, hd * 2 + cc, mc * 128:(mc + 1) * 128], qm[:, hd * 2 + cc, :],
                        start=(cc == 0), stop=(cc == 1)), waits=[t_qm] + list(k.brd[sbk]), sig=(cc == 1))
                te = p.op("act", lambda e, mc=mc, pm=pm, sbk=sbk: e.activation(out=pm[:, mc, :], in_=ps[:, sbk, :], func=AF.Exp,
                                                                               scale=1.0 / 16.0),
                          waits=[tS] + pm_free[hd % 2] + tile_free)
                k.brd[sbk] = [te]
                tE.append(te)
            tO = []
            for cc in range(2):
                t_ = None
                for mc in range(2):
                    t_ = p.op("pe", lambda e, mc=mc, cc=cc, hd=hd, pm=pm: e.matmul(
                        ps[:, 2 + cc, :], vmem[:, mc, hd * 256 + cc * 128:hd * 256 + (cc + 1) * 128], pm[:, mc, :],
                        start=(mc == 0), stop=(mc == 1)), waits=tE + list(k.brd[2 + cc]), sig=(mc == 1))
                tO.append(t_)
            tD = None
            for mc in range(2):
                tD = p.op("pe", lambda e, mc=mc, pm=pm: e.matmul(ps[:, 6, :], ones16[:, :], pm[:, mc, :],
                                                               start=(mc == 0), stop=(mc == 1)),
                          waits=tE + list(k.brd[6]), sig=(mc == 1))
            pm_free[hd % 2] = [tD]
            t_r = p.op("dve", lambda e: e.reciprocal(out=rden[:, :], in_=ps[:, 6, :]), waits=[tD] + tile_free)
            k.brd[6] = [t_r]
            for cc in range(2):
                t_o = p.op("dve", lambda e, cc=cc, hd=hd: e.tensor_tensor(
                    out=ymem[:, hd * 2 + cc, :], in0=ps[:, 2 + cc, :], in1=rden[:, :], op=ALU.mult),
                    waits=[tO[cc], t_r])
                k.brd[2 + cc] = [t_o]
        mix_ready = p.bar()
        brs = [(W["w_br_attn"], 4, ya), (W["w_br_pool"], 8, ypool), (W["w_br_mem"], 8, ymem)]
        first = True
        for br, (wbr, kcb, ybr) in enumerate(brs):
            for cb in range(8):
                def evac_gate(banks, pe_tok, br=br, cb=cb):
                    i = k.tmpg_i % 2
                    k.tmpg_i += 1
                    k.cur_tmpg = i
                    toks = []
                    for ch in range(2):
                        col = br * 16 + cb * 2 + ch
                        toks.append(p.op("act", lambda e, ch=ch, i=i, col=col: e.activation(
                            out=tmpg[i][:, ch, :], in_=ps[:, banks[ch], :], func=AF.Sigmoid, bias=vcol(V_BGATE, col)),
                            waits=[pe_tok] + list(k.tmpg_free[i])))
                    k.tmpg_tok = toks[-1]
                    return toks

                def evac_br(banks, pe_tok, br=br, cb=cb):
                    i = k.cur_tmpg
                    toks = []
                    for ch in range(2):
                        c = cb * 2 + ch
                        if br == 0:
                            t_ = p.op("dve", lambda e, ch=ch, i=i, c=c: e.tensor_tensor(
                                out=merged[:, c, :], in0=tmpg[i][:, ch, :], in1=ps[:, banks[ch], :], op=ALU.mult),
                                waits=[pe_tok, k.tmpg_tok] + mix_ready)
                            toks.append(t_)
                        else:
                            t1 = p.op("dve", lambda e, ch=ch, i=i: e.tensor_tensor(
                                out=tmpg[i][:, ch, :], in0=tmpg[i][:, ch, :], in1=ps[:, banks[ch], :], op=ALU.mult),
                                waits=[pe_tok, k.tmpg_tok])
                            t2 = p.op("dve", lambda e, ch=ch, i=i, c=c: e.tensor_tensor(
                                out=merged[:, c, :], in0=merged[:, c, :], in1=tmpg[i][:, ch, :], op=ALU.add),
                                waits=[t1, k.merged_tok[c]])
                            toks.append(t1)
                            t_ = t2
                        k.merged_tok[c] = t_
                        k.tmpg_free[i] = [t_]
                    return toks

                if not hasattr(k, "merged_tok"):
                    k.merged_tok = [None] * DC
                jg = brjobs[(br * 8 + cb) * 2]
                jg.update(evac=evac_gate, waits=([t_h] + mix_ready) if first else [])
                do_job(jg)
                first = False
                jb_ = brjobs[(br * 8 + cb) * 2 + 1]
                jb_.update(evac=evac_br, waits=[t_ya] + mix_ready)
                do_job(jb_)
        flush_deferred()
        pe_done = p.last["pe"]
        t_mb = None
        for c in range(DC):
            if c % 2:
                t_mb = p.op("act", lambda e, c=c: e.copy(out=hbf[:, c, :], in_=merged[:, c, :]),
                            waits=[pe_done, k.merged_tok[c]])
            else:
                t_mb = p.op("dve", lambda e, c=c: e.tensor_copy(hbf[:, c, :], merged[:, c, :]),
                            waits=[pe_done, k.merged_tok[c]])
        mb_toks = [p.last["act"], p.last["dve"]]
        for j in mk_jobs(W["w_out"], 16, [(cb * 256, 256) for cb in range(8)], lambda kc: hbf[:, kc, :], NT, "A",
                         out_evac_for(DC), waits=mb_toks):
            do_job(j)
        t_x2 = post_residual(V_MIXPOST)
        t_x3 = ffn(W["ffn2_w_up"], W["ffn2_w_down"], V_F2PRE, V_F2POST, [t_x2], cw=list(k.res_cw))
        fcw = list(k.res_cw)
        for c in range(DC):
            _, pe_part = stats_add(xres[:, c, :], NT, c == 0, c == DC - 1, [fcw[c]], on="act")
            pe_part()
        r, t_r = compute_rstd(NT)
        t_f = None
        for c in range(DC):
            t_f = p.op("dve", lambda e, c=c, r=r: e.scalar_tensor_tensor(
                out=fin[:, c, :], in0=xres[:, c, :], scalar=vcol(V_FINAL, c), in1=r[:, :],
                op0=ALU.mult, op1=ALU.mult), waits=[t_r, t_x3])
        xres_free2 = [t_f, p.last["pe"], p.last["act"]]
        if ti + 1 < NTILES:
            t_x_next = p.dma("sp", xres[:, :, :], x1T.rearrange("(c p) t -> p c t", p=128)[:, :, t0 + NT:t0 + 2 * NT], "ldx",
                             waits=xres_free2)
        cp_toks = []
        bi_ = 0
        for tb in range(4):
            for c4 in range(4):
                bank = bi_ % 8
                bi_ += 1
                tp = None
                for cc in range(4):
                    c = c4 * 4 + cc
                    tp = p.op("pe", lambda e, c=c, tb=tb, cc=cc, bank=bank: e.transpose(
                        ps[:, bank, cc * 128:(cc + 1) * 128], fin[:, c, tb * 128:(tb + 1) * 128], ident[:, :]),
                        waits=[t_f] + list(k.brd[bank]), sig=(cc == 3))
                if bi_ % 2:
                    tcp = p.op("act", lambda e, tb=tb, c4=c4, bank=bank: e.copy(
                        out=xout[:, tb, c4 * 512:(c4 + 1) * 512], in_=ps[:, bank, :]), waits=[tp, t_f])
                else:
                    tcp = p.op("dve", lambda e, tb=tb, c4=c4, bank=bank: e.tensor_copy(
                        xout[:, tb, c4 * 512:(c4 + 1) * 512], ps[:, bank, :]), waits=[tp, t_f])
                k.brd[bank] = [tcp]
                cp_toks.append(tcp)
        fin_free = [p.last["pe"]]
        t_out = p.dma("sp", y_d[t0:t0 + NT, :].rearrange("(b p) f -> p b f", p=128), xout[:, :, :], "yout",
                      waits=cp_toks[-2:])
        out_toks = [t_out]
        tile_free = p.bar() + [t_out]
        if stop == "PB1":
            writeback_pending()
            return finalize()

    writeback_pending()
    assert k.cache_next == NJ_PA + NJ_PB, k.cache_next
    p.wait("sp", out_toks)
    p.wait("sp", p.bar())
    p.emit()
    return nc, stack


_CACHE = {}


def _consts():
    s = 2.0 ** (-8.0 * np.arange(1, 13) / 12.0)
    slopes = s.reshape(4, 3).T.astype(np.float32)
    kk = np.arange(128)[:, None].astype(np.float64)
    qq = np.arange(128)[None, :].astype(np.float64)
    ab = np.full((128, 12, 256), -30000.0, np.float32)
    for g in range(3):
        for hh in range(4):
            sl = float(slopes[g, hh]) * DILS[g]
            lo = -sl * np.abs(kk - qq - 64)
            hi = -sl * np.abs(kk - qq + 64)
            ab[:, g * 4 + hh, 0:128] = np.where(kk >= qq, lo, -30000.0)
            ab[:, g * 4 + hh, 128:256] = np.where(kk <= qq, hi, -30000.0)
    return ab.reshape(128, 12 * 256)


def _pedge(tend_mid):
    pe = np.ones((4, 3, 8), np.float32)
    for g, w in enumerate(POOLW):
        hw = w // 2
        for i in range(8):
            t = i
            cnt = min(T, t - hw + w) - max(0, t - hw)
            pe[g, 0, i] = w / cnt
            t = T - 8 + i
            cnt = min(T, t - hw + w) - max(0, t - hw)
            pe[g, 2, i] = w / cnt
            if tend_mid is not None:
                t = tend_mid - 8 + i
                cnt = min(tend_mid, t - hw + w) - max(0, t - hw)
                pe[g, 1, i] = w / cnt
    return np.ascontiguousarray(np.broadcast_to(pe.reshape(1, 96), (128, 96)))


def _colpack(v):
    v = np.asarray(v, np.float32).reshape(-1)
    return v.reshape(-1, 128).T


def kernel(x_prompt, x_sample, mem_prompt, mem_sample,
           ffn1_norm_pre, ffn1_w_up, ffn1_w_down, ffn1_norm_post,
           mix_norm_pre, mem_norm, w_in, w_mem_kv, w_pool, pool_scale,
           w_br_attn, w_br_pool, w_br_mem, w_gate, b_gate, w_out, mix_norm_post,
           ffn2_norm_pre, ffn2_w_up, ffn2_w_down, ffn2_norm_post, final_norm):
    f32 = lambda a: np.ascontiguousarray(np.asarray(a, dtype=np.float32))
    x_prompt, x_sample, mem_prompt, mem_sample = map(f32, (x_prompt, x_sample, mem_prompt, mem_sample))
    vecs = np.ascontiguousarray(np.concatenate([
        _colpack(ffn1_norm_pre), _colpack(ffn1_norm_post), _colpack(mix_norm_pre), _colpack(mem_norm),
        _colpack(pool_scale), _colpack(b_gate), _colpack(mix_norm_post), _colpack(ffn2_norm_pre),
        _colpack(ffn2_norm_post), _colpack(final_norm)], axis=1).astype(np.float32))
    assert vecs.shape == (128, NV)
    shared = {
        "vecs": vecs, "abias": _consts(),
        "ffn1_w_up": f32(ffn1_w_up).reshape(D, 2 * DFF), "ffn1_w_down": f32(ffn1_w_down).reshape(DFF, D),
        "w_in": f32(w_in).reshape(D, 6656), "w_mem_kv": f32(w_mem_kv).reshape(D, 2048),
        "w_pool": f32(w_pool).reshape(1024, 256), "w_br_attn": f32(w_br_attn).reshape(512, D),
        "w_br_pool": f32(w_br_pool).reshape(1024, D), "w_br_mem": f32(w_br_mem).reshape(1024, D),
        "w_gate": f32(w_gate).reshape(D, 3 * D), "w_out": f32(w_out).reshape(D, D),
        "ffn2_w_up": f32(ffn2_w_up).reshape(D, 2 * DFF), "ffn2_w_down": f32(ffn2_w_down).reshape(DFF, D),
    }
    P_CORE = (0, 1, 4, 5)
    S_CORE = (2, 3, 6, 7)
    in_maps = [None] * 8
    for b in range(4):
        xp = np.zeros((T, D), np.float32)
        xp[:2048] = x_prompt[b]
        m = dict(shared)
        m.update({"x": xp, "mem": mem_prompt[b], "flag": np.zeros((128, 1), np.float32), "pedge": _pedge(2048)})
        in_maps[P_CORE[b]] = m
    for b in range(4):
        m = dict(shared)
        m.update({"x": x_sample[b], "mem": mem_sample[b], "flag": np.ones((128, 1), np.float32), "pedge": _pedge(None)})
        in_maps[S_CORE[b]] = m
    nc, stack = build_program()
    res = run_bass_kernel_spmd(nc, in_maps, core_ids=list(range(8)))
    y_prompt = np.stack([np.asarray(res.results[P_CORE[b]]["y"], np.float32)[:2048] for b in range(4)], axis=0)
    y_sample = np.stack([np.asarray(res.results[S_CORE[b]]["y"], np.float32) for b in range(4)], axis=0)
    return (y_prompt, y_sample)
```
